# Optimizing a Trainium2 kernel written in Bass

```python
import math
import jax, jax.numpy as jnp
from jax import lax
import numpy as np

D_MODEL = 2048
BATCH = 1
SEQ = 8192
DEPTH = 1
DEC_BATCH = 128
DEC_SEQ = 1
PAST_LEN = 8192
PAGE_SIZE = 128

HEAD_DIM = 64
N_HEADS = D_MODEL // HEAD_DIM
N_KV_HEADS = 4
GROUP = N_HEADS // N_KV_HEADS
Q_DIM = N_HEADS * HEAD_DIM
KV_DIM = N_KV_HEADS * HEAD_DIM
WINDOW = 128
ROPE_THETA = 10000.0
POOL_WINDOWS = (2, 4, 8, 16)
POOL_GROUPS = len(POOL_WINDOWS)
POOL_WIDTH = D_MODEL // 2
POOL_GROUP_DIM = POOL_WIDTH // POOL_GROUPS
POOL_HIST = max(POOL_WINDOWS) - 1
D_FF = 5632
CONV_W = 3
IN_DIM = Q_DIM + 2 * KV_DIM + POOL_WIDTH + 2 * D_MODEL
LN_EPS = 1e-5
NEG_INF = -1e30

kernel_name = "hybrid_pool_swa_sink_convffn_deepnorm_step"


def layer_norm(x, g, b):
    xf = x.astype(jnp.float32)
    mu = jnp.mean(xf, axis=-1, keepdims=True)
    var = jnp.mean(jnp.square(xf - mu), axis=-1, keepdims=True)
    return ((xf - mu) * lax.rsqrt(var + LN_EPS) * g.astype(jnp.float32) + b.astype(jnp.float32)).astype(x.dtype)


def rope(x, pos):
    half = HEAD_DIM // 2
    inv = ROPE_THETA ** (-jnp.arange(half, dtype=jnp.float32) / half)
    ang = pos.astype(jnp.float32)[:, None] * inv[None, :]
    cos = jnp.cos(ang)[:, None, :]
    sin = jnp.sin(ang)[:, None, :]
    x1 = x[..., :half].astype(jnp.float32)
    x2 = x[..., half:].astype(jnp.float32)
    return jnp.concatenate([x1 * cos - x2 * sin, x2 * cos + x1 * sin], axis=-1).astype(x.dtype)


def window_attention(q, k_ext, v_ext, pos0, sinks):
    N, T = q.shape[0], q.shape[1]
    blk = WINDOW if T % WINDOW == 0 else T
    nblk = T // blk
    qb = q.reshape(N, nblk, blk, N_KV_HEADS, GROUP, HEAD_DIM)
    if blk == WINDOW:
        kb = k_ext.reshape(N, nblk + 1, WINDOW, N_KV_HEADS, HEAD_DIM)
        vb = v_ext.reshape(N, nblk + 1, WINDOW, N_KV_HEADS, HEAD_DIM)
        kb = jnp.concatenate([kb[:, :-1], kb[:, 1:]], axis=2)
        vb = jnp.concatenate([vb[:, :-1], vb[:, 1:]], axis=2)
    else:
        kb = k_ext[:, None]
        vb = v_ext[:, None]
    q_pos = pos0 + jnp.arange(T).reshape(nblk, blk)
    k_pos = pos0 - WINDOW + jnp.arange(nblk)[:, None] * blk + jnp.arange(WINDOW + blk)[None, :]
    diff = q_pos[:, :, None] - k_pos[:, None, :]
    visible = (diff >= 0) & (diff < WINDOW) & (k_pos[:, None, :] >= 0)
    s = jnp.einsum('nbqkgd,nbskd->nbkgqs', qb, kb, preferred_element_type=jnp.float32) * (HEAD_DIM ** -0.5)
    s = jnp.where(visible[None, :, None, None], s, NEG_INF)
    sink = sinks.astype(jnp.float32).reshape(1, 1, N_KV_HEADS, GROUP, 1, 1)
    m = jnp.maximum(jnp.max(s, axis=-1, keepdims=True), sink)
    e = jnp.exp(s - m)
    p = e / (jnp.sum(e, axis=-1, keepdims=True) + jnp.exp(sink - m))
    o = jnp.einsum('nbkgqs,nbskd->nbqkgd', p.astype(v_ext.dtype), vb)
    return o.reshape(N, T, Q_DIM)


def pool_mix(u, hist, pos0, w_pool_mix, pool_scale):
    N, T = u.shape[0], u.shape[1]
    ext = jnp.concatenate([hist, u], axis=1).astype(jnp.float32)
    cs = jnp.pad(jnp.cumsum(ext, axis=1), ((0, 0), (1, 0), (0, 0)))
    pos = pos0 + jnp.arange(T)
    groups = []
    for g, w in enumerate(POOL_WINDOWS):
        sl = slice(g * POOL_GROUP_DIM, (g + 1) * POOL_GROUP_DIM)
        tot = cs[:, POOL_HIST + 1:POOL_HIST + 1 + T, sl] - cs[:, POOL_HIST + 1 - w:POOL_HIST + 1 - w + T, sl]
        cnt = jnp.minimum(w, pos + 1).astype(jnp.float32)[:, None]
        groups.append(tot / cnt - u[..., sl].astype(jnp.float32))
    d = jnp.stack(groups, axis=2).astype(u.dtype)
    y = jnp.einsum('ntgc,gcd->ntgd', d, w_pool_mix).reshape(N, T, POOL_WIDTH)
    return y * pool_scale


def causal_dwconv(g, hist, conv_w, conv_b):
    T = g.shape[1]
    ext = jnp.concatenate([hist, g], axis=1)
    y = conv_b + ext[:, 0:T] * conv_w[0]
    for j in range(1, CONV_W):
        y = y + ext[:, j:j + T] * conv_w[j]
    return y, ext[:, -(CONV_W - 1):]


def decoder_layer(x, hist_k, hist_v, hist_pool, hist_conv, pos0,
                  w_in, attn_sinks, w_pool_mix, pool_scale, w_attn_branch, w_pool_branch, w_out,
                  ln1_g, ln1_b, w_up, w_gate, conv_w, conv_b, w_down, ln2_g, ln2_b):
    N, T, _ = x.shape
    alpha = (2.0 * DEPTH) ** 0.25
    pos = pos0 + jnp.arange(T)
    proj = x @ w_in
    cuts = [Q_DIM, Q_DIM + KV_DIM, Q_DIM + 2 * KV_DIM, Q_DIM + 2 * KV_DIM + POOL_WIDTH,
            Q_DIM + 2 * KV_DIM + POOL_WIDTH + D_MODEL]
    q, k, v, u, gate_pool, gate_attn = jnp.split(proj, cuts, axis=-1)
    q = rope(q.reshape(N, T, N_HEADS, HEAD_DIM), pos)
    k = rope(k.reshape(N, T, N_KV_HEADS, HEAD_DIM), pos)
    v = v.reshape(N, T, N_KV_HEADS, HEAD_DIM)
    k_ext = jnp.concatenate([hist_k, k], axis=1)
    v_ext = jnp.concatenate([hist_v, v], axis=1)
    attn = window_attention(q, k_ext, v_ext, pos0, attn_sinks)
    pooled = pool_mix(u, hist_pool, pos0, w_pool_mix, pool_scale)
    new_pool = jnp.concatenate([hist_pool, u], axis=1)[:, -POOL_HIST:]
    merged = (jax.nn.sigmoid(gate_pool) * (pooled @ w_pool_branch)
              + jax.nn.sigmoid(gate_attn) * (attn @ w_attn_branch))
    x1 = layer_norm(alpha * x + merged @ w_out, ln1_g, ln1_b)
    gc, new_conv = causal_dwconv(x1 @ w_gate, hist_conv, conv_w, conv_b)
    ffn = (jax.nn.gelu(gc) * (x1 @ w_up)) @ w_down
    x2 = layer_norm(alpha * x1 + ffn, ln2_g, ln2_b)
    return x2, k_ext[:, -WINDOW:], v_ext[:, -WINDOW:], new_pool, new_conv


def setup_inputs(seed: int = 0) -> dict:
    key = jax.random.key(seed)
    ks = jax.random.split(key, 24)
    f32 = jnp.float32
    beta = (8.0 * DEPTH) ** -0.25
    nrm = lambda k, shape, scale: jax.random.normal(k, shape, f32) * scale
    col_scale = jnp.concatenate([jnp.ones((Q_DIM + KV_DIM,), f32), jnp.full((KV_DIM,), beta, f32),
                                 jnp.ones((POOL_WIDTH + 2 * D_MODEL,), f32)])
    return {
        'x_prompt': nrm(ks[0], (BATCH, SEQ, D_MODEL), 1.0),
        'x_sample': nrm(ks[1], (DEC_BATCH, DEC_SEQ, D_MODEL), 1.0),
        'cache_k': nrm(ks[2], (DEPTH, DEC_BATCH, WINDOW, N_KV_HEADS, HEAD_DIM), 1.0),
        'cache_v': nrm(ks[3], (DEPTH, DEC_BATCH, WINDOW, N_KV_HEADS, HEAD_DIM), beta),
        'state_pool': nrm(ks[4], (DEPTH, DEC_BATCH, POOL_HIST, POOL_WIDTH), 1.0),
        'state_conv': nrm(ks[5], (DEPTH, DEC_BATCH, CONV_W - 1, D_FF), 1.0),
        'w_in': nrm(ks[6], (DEPTH, D_MODEL, IN_DIM), D_MODEL ** -0.5) * col_scale,
        'attn_sinks': nrm(ks[7], (DEPTH, N_HEADS), 1.0),
        'w_pool_mix': nrm(ks[8], (DEPTH, POOL_GROUPS, POOL_GROUP_DIM, POOL_GROUP_DIM), POOL_GROUP_DIM ** -0.5),
        'pool_scale': 1.0 + nrm(ks[9], (DEPTH, POOL_WIDTH), 0.1),
        'w_attn_branch': nrm(ks[10], (DEPTH, Q_DIM, D_MODEL), Q_DIM ** -0.5),
        'w_pool_branch': nrm(ks[11], (DEPTH, POOL_WIDTH, D_MODEL), POOL_WIDTH ** -0.5),
        'w_out': nrm(ks[12], (DEPTH, D_MODEL, D_MODEL), beta * D_MODEL ** -0.5),
        'ln1_g': 1.0 + nrm(ks[13], (DEPTH, D_MODEL), 0.02),
        'ln1_b': nrm(ks[14], (DEPTH, D_MODEL), 0.02),
        'w_up': nrm(ks[15], (DEPTH, D_MODEL, D_FF), D_MODEL ** -0.5),
        'w_gate': nrm(ks[16], (DEPTH, D_MODEL, D_FF), D_MODEL ** -0.5),
        'conv_w': nrm(ks[17], (DEPTH, CONV_W, D_FF), CONV_W ** -0.5),
        'conv_b': nrm(ks[18], (DEPTH, D_FF), 0.01),
        'w_down': nrm(ks[19], (DEPTH, D_FF, D_MODEL), beta * D_FF ** -0.5),
        'ln2_g': 1.0 + nrm(ks[20], (DEPTH, D_MODEL), 0.02),
        'ln2_b': nrm(ks[21], (DEPTH, D_MODEL), 0.02),
    }


def reference(x_prompt, x_sample, cache_k, cache_v, state_pool, state_conv,
              w_in, attn_sinks, w_pool_mix, pool_scale, w_attn_branch, w_pool_branch, w_out,
              ln1_g, ln1_b, w_up, w_gate, conv_w, conv_b, w_down, ln2_g, ln2_b):
    B = x_prompt.shape[0]
    dt = x_prompt.dtype
    hp, hs = x_prompt, x_sample
    kp_l, vp_l, pp_l, cp_l, ks_l, vs_l, ps_l, cs_l = [], [], [], [], [], [], [], []
    for l in range(DEPTH):
        weights = (w_in[l], attn_sinks[l], w_pool_mix[l], pool_scale[l], w_attn_branch[l], w_pool_branch[l],
                   w_out[l], ln1_g[l], ln1_b[l], w_up[l], w_gate[l], conv_w[l], conv_b[l], w_down[l],
                   ln2_g[l], ln2_b[l])
        hp, kp, vp, pp, cp = decoder_layer(
            hp,
            jnp.zeros((B, WINDOW, N_KV_HEADS, HEAD_DIM), dt),
            jnp.zeros((B, WINDOW, N_KV_HEADS, HEAD_DIM), dt),
            jnp.zeros((B, POOL_HIST, POOL_WIDTH), dt),
            jnp.zeros((B, CONV_W - 1, D_FF), dt),
            0, *weights)
        hs, ksm, vsm, psm, csm = decoder_layer(
            hs, cache_k[l], cache_v[l], state_pool[l], state_conv[l], PAST_LEN, *weights)
        kp_l.append(kp); vp_l.append(vp); pp_l.append(pp); cp_l.append(cp)
        ks_l.append(ksm); vs_l.append(vsm); ps_l.append(psm); cs_l.append(csm)
    return (hp, hs,
            jnp.stack(kp_l), jnp.stack(vp_l), jnp.stack(pp_l), jnp.stack(cp_l),
            jnp.stack(ks_l), jnp.stack(vs_l), jnp.stack(ps_l), jnp.stack(cs_l))
```

```python
import numpy as np
import concourse.bass as bass
import concourse.mybir as mybir
from concourse.bass_utils import run_bass_kernel_spmd

F32 = mybir.dt.float32
BF16 = mybir.dt.bfloat16
AF = mybir.ActivationFunctionType
ALU = mybir.AluOpType

NCORES = 8
D = 2048
KC = 16
T = 1024
NS = 16
NB = 1042
NA = 1296
DFF = 5632
FC = 44
ALPHA = 2.0 ** 0.25
EPS = 1e-5
PIECES_B = [(0, 512), (512, 512), (1024, 18)]


def OP(meth, *a, **kw):
    return (meth, a, kw)


class Builder:
    def __init__(self, nc):
        self.nc = nc
        self.ops = []
        self.last_write = {}
        self.readers = {}
        self.last_sk = {}
        self.pending = {}

    def fence(self):
        f = set(self.last_sk.values())
        for q in ("pe", "act", "dve", "pool", "sp"):
            self.pending[q] = set(f) | self.pending.get(q, set())

    def add(self, q, fn, reads=(), writes=(), dma=None, nofence=False):
        if q == "pe":
            meth, a, kw = fn
            opnd = kw["lhsT"] if meth == "matmul" else a[1]
            if opnd.dtype == F32:
                self.pe_f32 = True
            else:
                if getattr(self, "pe_f32", False) and opnd.shape[-1] == 128 and len(opnd.shape) == 2:
                    out = a[0]
                    assert len(out.shape) == 2 and (meth != "matmul" or kw.get("start", True)), (meth, out.shape)
                    self.pe_f32 = False
                    self.add("pe", OP("matmul", out[0:32, 0:1], lhsT=self.dummy_w[:, 0:32], rhs=self.dummy_w[:, 0:1],
                                      start=True, stop=True), reads=list(reads), writes=list(writes))
                self.pe_f32 = False
        idx = len(self.ops)
        deps = set()
        for r in reads:
            if r in self.last_write:
                deps.add(self.last_write[r])
            if isinstance(r, tuple) and r[0] == "bk":
                for sk2, rd in self.readers.get(r, {}).items():
                    if sk2 != ("dma", dma) and sk2 != ("eng", q):
                        deps.add(rd)
        for w in writes:
            if w in self.last_write:
                deps.add(self.last_write[w])
            for rd in self.readers.get(w, {}).values():
                deps.add(rd)
        sk = ("dma", dma) if dma is not None else ("eng", q)
        if self.pending.get(q) and not nofence:
            deps |= self.pending[q]
            self.pending[q] = set()
        self.last_sk[sk] = idx
        for r in reads:
            self.readers.setdefault(r, {})[sk] = idx
        for w in writes:
            self.last_write[w] = idx
            self.readers[w] = {}
        deps.discard(idx)
        self.ops.append(dict(q=q, fn=fn, deps=deps, dma=dma, sk=sk, signal=dma is not None, val=None))
        return idx

    def plan(self):
        ops = self.ops
        for op in ops:
            need = {}
            for d in op["deps"]:
                dop = ops[d]
                if dop["sk"] == ("eng", "pe") and op["sk"] == ("eng", "pe"):
                    continue
                k = dop["sk"]
                if k not in need or need[k] < d:
                    need[k] = d
            op["need"] = need
            for d in need.values():
                ops[d]["signal"] = True
        cnt = {}
        for op in ops:
            if op["signal"]:
                inc = 16 if op["dma"] is not None else 1
                cnt[op["sk"]] = cnt.get(op["sk"], 0) + inc
                op["val"] = cnt[op["sk"]]
        self.final = cnt

    def emit(self, block_engines, sems):
        ops = self.ops
        for q, eng in block_engines.items():
            waited = {}
            for op in ops:
                if op["q"] != q:
                    continue
                for k, d in op["need"].items():
                    v = ops[d]["val"]
                    if waited.get(k, 0) >= v:
                        continue
                    eng.wait_ge(sems[k], v)
                    waited[k] = v
                meth, a, kw = op["fn"]
                inst = getattr(eng, meth)(*a, **kw)
                if op["signal"]:
                    inst.then_inc(sems[op["sk"]], 16 if op["dma"] is not None else 1)
            if q == "sp":
                for k, v in self.final.items():
                    if k[0] == "dma":
                        eng.wait_ge(sems[k], v)


def build_program(stop=None):
    nc = bass.Bass("TRN2", target_bir_lowering=False)
    B = Builder(nc)

    def finish():
        _emit_program(nc, B)
        release(0)
        return nc

    def din(name, shape, dt=F32):
        return nc.dram_tensor(name, list(shape), dt, kind="ExternalInput").ap()

    def dout(name, shape):
        return nc.dram_tensor(name, list(shape), F32, kind="ExternalOutput").ap()

    xin = din("xin", [NA, D])
    cache_k = din("cache_k", [NS, 128, 256])
    cache_v = din("cache_v", [NS, 128, 256])
    state_pool = din("state_pool", [NS, 15, 1024])
    state_conv = din("state_conv", [NS, 2, DFF])
    w_q = din("w_q", [D, 2048])
    w_kd = din("w_kd", [D, 512])
    w_v = din("w_v", [D, 256])
    w_u = din("w_u", [D, 1024])
    w_gp = din("w_gp", [D, 2048])
    w_ga = din("w_ga", [D, 2048])
    w_mix = din("w_mix", [4, 256, 256])
    w_ab = din("w_ab", [2048, 2048])
    w_pb = din("w_pb", [1024, 2048])
    w_o = din("w_o", [2048, 2048])
    w_up = din("w_up", [D, DFF])
    w_gate = din("w_gate", [D, DFF])
    w_down = din("w_down", [DFF, D])
    cosA_d = din("cosA", [128, NA])
    sinA_d = din("sinA", [128, NA])
    cosS_d = din("cosS", [128, 18])
    sinS_d = din("sinS", [128, 18])
    masks_d = din("masks", [128, 3, 128])
    sel_d = din("sel", [120, 4, 8])
    invc_d = din("invc", [128, 4, 16])
    cols_d = din("cols", [128, 256])
    rep_d = din("rep", [4, 128, D])
    consts_d = din("consts", [128, 3, 128])

    y_o = dout("y", [T + NS, D])
    kc_p_o = dout("kc_p", [128, 256])
    vc_p_o = dout("vc_p", [128, 256])
    pool_p_o = dout("pool_p", [16, 1024])
    conv_p_o = dout("conv_p", [2, DFF])
    kc_s_o = dout("kc_s", [NS, 128, 256])
    vc_s_o = dout("vc_s", [NS, 128, 256])
    pool_s_o = dout("pool_s", [NS, 15, 1024])
    conv_s_o = dout("conv_s", [NS, 2, DFF])

    x1_d = nc.dram_tensor("x1_scr", [NB, D], F32).ap()
    y2_d = nc.dram_tensor("y2_scr", [T + NS, D], F32).ap()
    kv_new_d = nc.dram_tensor("kvnew_scr", [2, NS, 256], F32).ap()
    mg_d = nc.dram_tensor("mg_scr", [KC, 128, NB], BF16).ap()

    C_PSCALE = 0
    C_SINK = 8
    C_CONVW = 24
    C_CONVB = 156
    C_HV = 200
    C_EPS = 201

    stack = []

    uniq = dict(n=0)

    def sb(name, shape, dt):
        uniq["n"] += 1
        cm = nc.sbuf_tensor(f"sb{uniq['n']}_{name}", list(shape), dt)
        t = cm.__enter__()
        stack.append(cm)
        return t

    def mark():
        return len(stack)

    def release(m):
        if len(stack) > m:
            B.fence()
        while len(stack) > m:
            stack.pop().__exit__(None, None, None)

    def ps(name):
        cm = nc.psum_tensor(name, [128, 512], F32)
        t = cm.__enter__()
        stack.append(cm)
        return t

    banks = [ps(f"bank{i}") for i in range(8)]

    consts32 = sb("consts32", [128, 3, 128], F32)
    ident_bf = sb("ident_bf", [128, 128], BF16)
    perm_bf = sb("perm_bf", [128, 128], BF16)
    ones_bf = sb("ones_bf", [128, 128], BF16)
    colsT = sb("colsT", [128, 256], F32)
    esink = sb("esink", [128, 16], F32)
    wring = [sb(f"wring{i}", [128, 8192], BF16) for i in range(3)]
    ident32 = consts32[:, 0, :]
    B.dummy_w = ident_bf

    wstate = dict(n=0, lru=[0, 1, 2])

    def take_slot():
        s = wstate["lru"].pop(0)
        wstate["lru"].append(s)
        return s

    def load_slab(W, row_kc, c0, cw, name):
        slot = take_slot()
        dst = wring[slot][:, 0:row_kc * cw].rearrange("p (k c) -> p k c", k=row_kc)
        src = W[:, c0:c0 + cw].rearrange("(k p) c -> p k c", p=128)
        B.add("pool", OP("dma_start", out=dst, in_=src),
              reads=[], writes=[("w", slot)], dma=f"w{slot}", nofence=(slot < 3))
        return slot, dst

    accn = dict(n=0)

    def next_group():
        gi = accn["n"] % 3
        sm = 6 + (accn["n"] % 2)
        accn["n"] += 1
        return [(banks[2 * gi], 0, ("bk", 2 * gi)), (banks[2 * gi + 1], 0, ("bk", 2 * gi + 1)),
                (banks[sm], 0, ("bk", sm))]

    def mm_acc(grp, slot, slab, kcs, mcol, msz, src, src_res, pieces=PIECES_B, src_col=lambda c: c):
        for sel in (grp[0:2], grp[2:3]):
            psel = pieces[0:2] if sel is not grp[2:3] and len(sel) == 2 else pieces[2:3]
            for k in range(kcs):
                for (bank, off, res), (c0, n) in zip(sel, psel):
                    B.add("pe", OP("matmul",
                        bank[0:msz, off:off + n], lhsT=slab[:, k, mcol:mcol + msz], rhs=src[:, k, src_col(c0):src_col(c0) + n],
                        start=(k == 0), stop=(k == kcs - 1)),
                        reads=[("w", slot)] + src_res, writes=[res])

    B.add("sp", OP("dma_start", out=consts32[:], in_=consts_d), writes=["consts32"], dma="c0")
    B.add("sp", OP("dma_start", out=colsT[:], in_=cols_d), writes=["colsT"], dma="c1")
    B.add("dve", OP("tensor_copy", out=ident_bf[:], in_=consts32[:, 0, :]), reads=["consts32"], writes=["ident_bf"])
    B.add("dve", OP("tensor_copy", out=perm_bf[:], in_=consts32[:, 1, :]), reads=["consts32"], writes=["perm_bf"])
    B.add("dve", OP("tensor_copy", out=ones_bf[:], in_=consts32[:, 2, :]), reads=["consts32"], writes=["ones_bf"])
    B.add("act", OP("activation", out=esink[:], in_=colsT[:, C_SINK:C_SINK + 16], func=AF.Exp),
          reads=["colsT"], writes=["esink"])

    if stop == "s":
        return finish()
    m_phase1 = mark()
    xT = sb("xT", [128, KC, NB], BF16)
    attnT = sb("attnT", [128, KC, NB], BF16)
    xThb = sb("xThb", [128, KC, 128], BF16)
    qS = sb("qS", [128, KC, NS], BF16)
    dtmp = [sb(f"dtmp{i}", [128, 512], F32) for i in range(2)]
    m_attn = mark()
    xTha = sb("xTha", [128, KC, 128], BF16)
    kT = sb("kT", [128, 4, NA], BF16)
    Vtok = sb("Vtok", [128, 10, 256], BF16)
    vS32 = sb("vS32", [16, 256], F32)
    cosA = sb("cosA_t", [128, NA], F32)
    sinA = sb("sinA_t", [128, NA], F32)
    cosS = sb("cosS_t", [128, 18], F32)
    sinS = sb("sinS_t", [128, 18], F32)
    masks32 = sb("masks32", [128, 3, 128], F32)
    masks = sb("masks_bf", [128, 3, 128], BF16)
    B.add("sp", OP("dma_start", out=cosA[:], in_=cosA_d), writes=["cosA"], dma="c2")
    B.add("sp", OP("dma_start", out=sinA[:], in_=sinA_d), writes=["sinA"], dma="c3")
    B.add("sp", OP("dma_start", out=cosS[:], in_=cosS_d), writes=["cosS"], dma="c4")
    B.add("sp", OP("dma_start", out=sinS[:], in_=sinS_d), writes=["sinS"], dma="c5")
    B.add("sp", OP("dma_start", out=masks32[:], in_=masks_d), writes=["masks32"], dma="c6")
    B.add("dve", OP("tensor_copy", out=masks[:], in_=masks32[:]), reads=["masks32"], writes=["masks"])

    slot_k, slab_k = load_slab(w_kd, KC, 0, 512, "wk")
    slot_v, slab_v = load_slab(w_v, KC, 0, 256, "wv")

    m_p0 = mark()
    xtok = [sb(f"xtok{i}", [128, D], F32) for i in range(2)]
    xtiles = [(0, 128, [(0, 128, xTha, 0)]),
              (128, 128, [(0, 128, xThb, 0), (126, 2, xT, 1024)])]
    for i in range(8):
        xtiles.append((256 + 128 * i, 128, [(0, 128, xT, 128 * i)]))
    xtiles.append((1280, 16, [(0, 16, xT, 1026)]))
    import os
    if os.environ.get("SKIP_P0"):
        xtiles = []
        B.add("dve", OP("memset", xT[:], 0.5), writes=[("xT", q) for q in range(4)])
        B.add("dve", OP("memset", xTha[:], 0.5), writes=[("xTha", q) for q in range(4)])
        B.add("dve", OP("memset", xThb[:], 0.5), writes=[("xThb", q) for q in range(4)])
    for ti, (r0, nr, dsts) in enumerate(xtiles):
        xb = xtok[ti % 2]
        B.add("sp", OP("dma_start", out=xb[0:nr, :], in_=xin[r0:r0 + nr, :]),
              writes=[("xtok", ti % 2)], dma=f"xtok{ti % 2}")
        for qd in range(4):
            bank = banks[(ti * 4 + qd) % 8]
            bres = ("bk", (ti * 4 + qd) % 8)
            extra = []
            for j in range(4):
                kc = qd * 4 + j
                B.add("pe", OP("transpose",
                    bank[:, j * 128:j * 128 + nr], xb[0:nr, kc * 128:(kc + 1) * 128], ident32[0:nr, 0:nr]),
                    reads=[("xtok", ti % 2), "consts32"], writes=[bres] + extra)
            for (s0, n, dst, dc) in dsts:
                eng = "act" if (qd % 2 == 0) else "dve"
                src = bank[:, :].rearrange("p (j c) -> p j c", j=4)[:, :, s0:s0 + n]
                dd = dst[:, qd * 4:qd * 4 + 4, dc:dc + n]
                dres = ("xT", qd) if dst is xT else (("xTha", qd) if dst is xTha else ("xThb", qd))
                if eng == "act":
                    B.add("act", OP("copy", out=dd, in_=src), reads=[bres], writes=[dres])
                else:
                    B.add("dve", OP("tensor_copy", out=dd, in_=src), reads=[bres], writes=[dres])
    if stop == "0":
        return finish()
    release(m_p0)
    xT_res = [("xT", q) for q in range(4)]
    xTha_res = [("xTha", q) for q in range(4)]
    xThb_res = [("xThb", q) for q in range(4)]

    ropetmp = dict(n=0)
    import os
    ROPE_DBG = int(os.environ.get("ROPE_DBG", "9"))
    raw_bf = [sb(f"raw_bf{i}", [128, 512], BF16) for i in range(4)]
    t1b = [sb(f"t1b{i}", [128, 512], F32) for i in range(4)]
    t2b = [sb(f"t2b{i}", [128, 512], F32) for i in range(2)]

    def rope_chunk(items):
        if ROPE_DBG == 0:
            for it in items:
                rope_piece(*it)
            return
        for i, (bank, off, res, n, cos_ap, sin_ap, dst_ap, dst_res, tab_res) in enumerate(items):
            src = bank[:, off:off + n]
            B.add("act", OP("copy", out=raw_bf[i][:, 0:n], in_=src), reads=[res], writes=[("raw", i)])
        for i, (bank, off, res, n, cos_ap, sin_ap, dst_ap, dst_res, tab_res) in enumerate(items):
            src = bank[:, off:off + n]
            B.add("dve", OP("tensor_tensor", out=t1b[i][:, 0:n], in0=src, in1=cos_ap, op=ALU.mult),
                  reads=[res] + tab_res, writes=[("t1", i)])
        for i, (bank, off, res, n, cos_ap, sin_ap, dst_ap, dst_res, tab_res) in enumerate(items):
            src = bank[:, off:off + n]
            B.add("pe", OP("matmul", src, lhsT=perm_bf[:], rhs=raw_bf[i][:, 0:n], start=True, stop=True),
                  reads=[("raw", i), "perm_bf"], writes=[res])
        for i, (bank, off, res, n, cos_ap, sin_ap, dst_ap, dst_res, tab_res) in enumerate(items):
            src = bank[:, off:off + n]
            t2 = t2b[i % 2]
            B.add("dve", OP("tensor_tensor", out=t2[:, 0:n], in0=src, in1=sin_ap, op=ALU.mult),
                  reads=[res] + tab_res, writes=[("t2", i % 2)])
            B.add("dve", OP("tensor_tensor", out=dst_ap, in0=t1b[i][:, 0:n], in1=t2[:, 0:n], op=ALU.add),
                  reads=[("t1", i), ("t2", i % 2)], writes=[dst_res])


    def rope_piece(bank, off, res, n, cos_ap, sin_ap, dst_ap, dst_res, tab_res):
        if ROPE_DBG == 0:
            B.add("act", OP("copy", out=dst_ap, in_=bank[:, off:off + n]), reads=[res], writes=[dst_res])
            return
        i = ropetmp["n"] % 2
        ropetmp["n"] += 1
        raw, t1, t2 = raw_bf[i], t1b[i], t2b[i]
        src = bank[:, off:off + n]
        B.add("act", OP("copy", out=raw[:, 0:n], in_=src), reads=[res], writes=[("raw", i)])
        B.add("dve", OP("tensor_tensor", out=t1[:, 0:n], in0=src, in1=cos_ap, op=ALU.mult),
              reads=[res] + tab_res, writes=[("t1", i)])
        B.add("pe", OP("matmul", src, lhsT=perm_bf[:], rhs=raw[:, 0:n], start=True, stop=True),
              reads=[("raw", i), "perm_bf"], writes=[res])
        B.add("dve", OP("tensor_tensor", out=t2[:, 0:n], in0=src, in1=sin_ap, op=ALU.mult),
              reads=[res] + tab_res, writes=[("t2", i)])
        B.add("dve", OP("tensor_tensor", out=dst_ap, in0=t1[:, 0:n], in1=t2[:, 0:n], op=ALU.add),
              reads=[("t1", i), ("t2", i)], writes=[dst_res])

    tabA = ["cosA", "sinA"]
    for g in range(4):
        grp = next_group()
        hbi = 13 - grp[2][2][1]
        hb, hres = banks[hbi], ("bk", hbi)
        for hsrc, hsres, hc in ((xTha, xTha_res, 0), (xThb, xThb_res, 128)):
            for k in range(KC):
                B.add("pe", OP("matmul",
                    hb[:, hc:hc + 128], lhsT=slab_k[:, k, g * 128:(g + 1) * 128], rhs=hsrc[:, k, 0:128],
                    start=(k == 0), stop=(k == KC - 1)), reads=[("w", slot_k)] + hsres, writes=[hres])
        for k in range(KC):
            for (bank, off, res), (c0, n) in zip(grp, [(0, 512), (512, 512), (1026, 16)]):
                B.add("pe", OP("matmul",
                    bank[:, off:off + n], lhsT=slab_k[:, k, g * 128:(g + 1) * 128], rhs=xT[:, k, c0:c0 + n],
                    start=(k == 0), stop=(k == KC - 1)), reads=[("w", slot_k)] + xT_res, writes=[res])
        pend_k = [(hb, 0, hres, 256, cosA[:, 0:256], sinA[:, 0:256], kT[:, g, 0:256], ("kT", g), tabA)]
        for (bank, off, res), (c0, n) in zip(grp, [(256, 512), (768, 512), (1280, 16)]):
            pend_k.append((bank, off, res, n, cosA[:, c0:c0 + n], sinA[:, c0:c0 + n], kT[:, g, c0:c0 + n], ("kT", g), tabA))
        rope_chunk(pend_k)
    kT_res = [("kT", g) for g in range(4)]
    if stop == "1a_k":
        return finish()

    vsrc = [(xTha, 0, 128, xTha_res), (xThb, 0, 128, xThb_res)] + [(xT, 128 * i, 128, xT_res) for i in range(8)] + \
           [(xT, 1026, 16, xT_res)]
    for j, (src, c0, n, sres) in enumerate(vsrc):
        bi = j % 2
        bank, bres = banks[bi], ("bk", bi)
        for k in range(KC):
            B.add("pe", OP("matmul",
                bank[0:n, 0:256], lhsT=src[:, k, c0:c0 + n], rhs=slab_v[:, k, 0:256], start=(k == 0), stop=(k == KC - 1)),
                reads=[("w", slot_v)] + sres, writes=[bres])
        if j < 10:
            B.add("act", OP("copy", out=Vtok[:, j, :], in_=bank[:, 0:256]), reads=[bres], writes=[("Vtok", j)])
        else:
            B.add("act", OP("copy", out=vS32[:, :], in_=bank[0:16, 0:256]), reads=[bres], writes=["vS32"])
        if j == 9:
            vst = sb("vst", [128, 256], F32)
            B.add("dve", OP("tensor_copy", out=vst[:], in_=bank[:, 0:256]), reads=[bres], writes=["vst"])
            B.add("sp", OP("dma_start", out=vc_p_o, in_=vst[:]), reads=["vst"], dma="o_vcp")
    B.add("sp", OP("dma_start", out=kv_new_d[1], in_=vS32[:]), reads=["vS32"], writes=["kvnew_v"], dma="kvn_v")

    kst = sb("kst", [128, 256], F32)
    kS32 = sb("kS32", [16, 256], F32)
    tb = banks[2][:, :].bitcast(BF16)
    for g in range(4):
        B.add("pe", OP("transpose", tb[:, g * 128:(g + 1) * 128], kT[:, g, 1152:1280], ident_bf[:]),
              reads=[("kT", g), "ident_bf"], writes=[("bk", 2)])
    B.add("act", OP("copy", out=kst[:].rearrange("p (g d) -> p g d", g=4),
                                  in_=tb[:, 0:512].rearrange("p (g d) -> p g d", g=4)[:, :, 0:64]),
          reads=[("bk", 2)], writes=["kst"])
    B.add("sp", OP("dma_start", out=kc_p_o, in_=kst[:]), reads=["kst"], dma="o_kcp")
    tb3 = banks[3][:, :].bitcast(BF16)
    for g in range(4):
        B.add("pe", OP("transpose", tb3[0:16, g * 128:(g + 1) * 128], kT[:, g, 1280:1296], ident_bf[:]),
              reads=[("kT", g), "ident_bf"], writes=[("bk", 3)])
    B.add("act", OP("copy", out=kS32[:].rearrange("p (g d) -> p g d", g=4),
                                  in_=tb3[0:16, 0:512].rearrange("p (g d) -> p g d", g=4)[:, :, 0:64]),
          reads=[("bk", 3)], writes=["kS32"])
    B.add("sp", OP("dma_start", out=kv_new_d[0], in_=kS32[:]), reads=["kS32"], writes=["kvnew_k"], dma="kvn_k")

    if stop == "1a":
        return finish()
    qT = sb("qT", [128, 4, NB], BF16)
    PT = [sb(f"PT{i}", [128, 512], BF16) for i in range(8)]
    ptn = dict(n=0)
    tabS = ["cosS", "sinS"]

    def attn_S(g, qc0, nq, ktiles, mask_ids, mcol0, blk, half, sbn):
        N = 4 * nq
        hs = slice(half * 64, half * 64 + 64)
        pts = []
        for ki, (kt, mid) in enumerate(zip(ktiles, mask_ids)):
            bi = (sbn % 2) * 2 + ki
            sbk, sres = banks[bi], ("bk", bi)
            pi = (sbn % 4) * 2 + ki
            pt = PT[pi]
            pts.append((ki, kt, pi))
            sout = sbk[:, 0:N].rearrange("p (c q) -> p c q", c=4)
            B.add("pe", OP("matmul", sout, lhsT=kT[hs, g, kt * 128:(kt + 1) * 128], rhs=qT[hs, 0:4, qc0:qc0 + nq], start=True, stop=True),
                  reads=[("kT", g), ("qT", 0)], writes=[sres])
            B.add("act", OP("activation", out=pt[:, 0:N], in_=sbk[:, 0:N], func=AF.Exp, scale=0.125),
                  reads=[sres], writes=[("PT", pi)])
            mk = masks[:, mid, mcol0:mcol0 + nq].unsqueeze(1).broadcast_to([128, 4, nq])
            ptv = pt[:, 0:N].rearrange("p (c q) -> p c q", c=4)
            B.add("dve", OP("tensor_tensor", out=ptv, in0=ptv, in1=mk, op=ALU.mult),
                  reads=[("PT", pi), "masks"], writes=[("PT", pi)])
        return pts

    def attn_PV(g, nq, blk, half, pts):
        N = 4 * nq
        hs = slice(half * 64, half * 64 + 64)
        nbi, dbi = 4 + blk % 2, 6 + blk % 2
        for (ki, kt, pi) in pts:
            B.add("pe", OP("matmul", banks[nbi][hs, 0:N], lhsT=Vtok[:, kt, g * 64:(g + 1) * 64], rhs=PT[pi][:, 0:N], start=(ki == 0), stop=(ki == 1)),
                  reads=[("PT", pi), ("Vtok", kt)], writes=[("bk", nbi)])
        for (ki, kt, pi) in pts:
            B.add("pe", OP("matmul", banks[dbi][hs, 0:N], lhsT=ones_bf[:, 0:64], rhs=PT[pi][:, 0:N], start=(ki == 0), stop=(ki == 1)),
                  reads=[("PT", pi), "ones_bf"], writes=[("bk", dbi)])

    def attn_norm(g, qc0, nq, blk):
        N = 4 * nq
        nbi, dbi = 4 + blk % 2, 6 + blk % 2
        nb, db = banks[nbi], banks[dbi]
        dt_ = dtmp[blk % 2]
        dtr = ("dtmp", blk % 2)
        B.add("dve", OP("tensor_tensor", out=dt_[:, 0:N].rearrange("p (c q) -> p c q", c=4),
                        in0=db[:, 0:N].rearrange("p (c q) -> p c q", c=4),
                        in1=esink[:, 4 * g:4 * g + 4].unsqueeze(2).broadcast_to([128, 4, nq]), op=ALU.add),
              reads=[("bk", dbi), "esink"], writes=[dtr])
        B.add("dve", OP("reciprocal", out=dt_[:, 0:N], in_=dt_[:, 0:N]), reads=[dtr], writes=[dtr])
        B.add("dve", OP("tensor_tensor", out=attnT[:, 4 * g:4 * g + 4, qc0:qc0 + nq],
                        in0=nb[:, 0:N].rearrange("p (c q) -> p c q", c=4),
                        in1=dt_[:, 0:N].rearrange("p (c q) -> p c q", c=4), op=ALU.mult),
              reads=[("bk", nbi), dtr], writes=[("attnT", g)])

    blk = 0
    for g in range(4):
        slot_q, slab_q = load_slab(w_q, KC, 512 * g, 512, "wq")
        pend_rope = None
        for c in range(4):
            grp = next_group()
            mm_acc(grp, slot_q, slab_q, KC, c * 128, 128, xT, xT_res)
            if pend_rope is not None:
                rope_chunk(pend_rope[0])
                B.add("act", OP("copy", out=qS[:, 4 * g + pend_rope[1], :], in_=qT[:, pend_rope[1], 1026:1042]),
                      reads=[("qT", 0)], writes=["qS"])
            items = []
            for pi_, ((bank, off, res), (c0, n)) in enumerate(zip(grp, PIECES_B)):
                if pi_ < 2:
                    items.append((bank, off, res, n, cosA[:, 256 + c0:256 + c0 + n], sinA[:, 256 + c0:256 + c0 + n],
                                  qT[:, c, c0:c0 + n], ("qT", 0), tabA))
                else:
                    items.append((bank, off, res, n, cosS[:, 0:18], sinS[:, 0:18], qT[:, c, c0:c0 + n], ("qT", 0), tabS))
            pend_rope = (items, c)
        rope_chunk(pend_rope[0])
        B.add("act", OP("copy", out=qS[:, 4 * g + pend_rope[1], :], in_=qT[:, pend_rope[1], 1026:1042]),
              reads=[("qT", 0)], writes=["qS"])
        blocks = [(128 * i, 128, (i + 1, i + 2), (2 if i == 0 else 1, 0), 0) for i in range(8)] + [(1024, 2, (0, 1), (1, 0), 126)]
        subs = [(bi_, half) for bi_ in range(len(blocks)) for half in range(2)]
        pend = {}
        for sn in range(len(subs) + 1):
            if sn < len(subs):
                bi_, half = subs[sn]
                qc0, nq, kts, mids, mc0 = blocks[bi_]
                pend[sn] = attn_S(g, qc0, nq, kts, mids, mc0, blk + bi_, half, sn)
            if sn >= 1:
                pb, ph = subs[sn - 1]
                qc0p, nqp = blocks[pb][0], blocks[pb][1]
                attn_PV(g, nqp, blk + pb, ph, pend.pop(sn - 1))
                if ph == 1:
                    attn_norm(g, qc0p, nqp, blk + pb)
        blk += len(blocks)

    if stop == "1b":
        return finish()
    release(m_attn)
    attn_res = [("attnT", g) for g in range(4)]
    m_sa = mark()
    H8 = 8
    Kc = sb("Kc", [128, H8, 256], F32)
    Vc = sb("Vc", [128, H8, 256], F32)
    Kcb = sb("Kcb", [128, H8, 4, 128], BF16)
    Vcb = sb("Vcb", [128, H8, 256], BF16)
    KnT = sb("KnT", [128, H8, 4, 128], BF16)
    PTs = sb("PTs", [128, 256], BF16)
    B.add("sp", OP("dma_start", out=kc_s_o[:, 0:127, :], in_=cache_k[:, 1:128, :]), dma="o_kcs")
    B.add("sp", OP("dma_start", out=vc_s_o[:, 0:127, :], in_=cache_v[:, 1:128, :]), dma="o_vcs")
    B.add("sp", OP("dma_start", out=kc_s_o[:, 127, :], in_=kv_new_d[0]), reads=["kvnew_k"], dma="o_kcs2")
    B.add("sp", OP("dma_start", out=vc_s_o[:, 127, :], in_=kv_new_d[1]), reads=["kvnew_v"], dma="o_vcs2")
    for hh in range(2):
        n0 = hh * H8
        B.add("sp", OP("dma_start", out=Kc[:, :, :], in_=cache_k[n0:n0 + H8, :, :].rearrange("n k f -> k n f")),
              writes=["Kc_a"], dma="kc_a")
        B.add("sp", OP("dma_start", out=Kc[0:1, :, :], in_=kv_new_d[0:1, n0:n0 + H8, :]),
              reads=["kvnew_k", "Kc_a"], writes=["Kc_a", "Kc_b"], dma="kc_b")
        B.add("sp", OP("dma_start", out=Vc[:, :, :], in_=cache_v[n0:n0 + H8, :, :].rearrange("n k f -> k n f")),
              writes=["Vc_a"], dma="vc_a")
        B.add("sp", OP("dma_start", out=Vc[0:1, :, :], in_=kv_new_d[1:2, n0:n0 + H8, :]),
              reads=["kvnew_v", "Vc_a"], writes=["Vc_a", "Vc_b"], dma="vc_b")
        Kc4 = Kc[:].rearrange("p n (g d) -> p n g d", g=4)
        B.add("act", OP("copy", out=Kcb[:, :, :, 0:64], in_=Kc4), reads=["Kc_a", "Kc_b"], writes=["Kcb0"])
        B.add("dve", OP("tensor_copy", out=Kcb[:, :, :, 64:128], in_=Kc4), reads=["Kc_a", "Kc_b"], writes=["Kcb1"])
        B.add("act", OP("copy", out=Vcb[:], in_=Vc[:]), reads=["Vc_a", "Vc_b"], writes=["Vcb"])
        for n in range(H8):
            bi = n % 2
            tbn = banks[bi][:, :].bitcast(BF16)
            for g in range(4):
                B.add("pe", OP("transpose", tbn[:, g * 128:(g + 1) * 128], Kcb[:, n, g, :], ident_bf[:]),
                      reads=["Kcb0", "Kcb1", "ident_bf"], writes=[("bk", bi)])
            srcv = tbn[:, 0:512].rearrange("p (g k) -> p g k", g=4)
            if n % 2 == 0:
                B.add("act", OP("copy", out=KnT[:, n, :, :], in_=srcv), reads=[("bk", bi)], writes=["KnT"])
            else:
                B.add("dve", OP("tensor_copy", out=KnT[:, n, :, :], in_=srcv), reads=[("bk", bi)], writes=["KnT"])
        sbk_h = [(banks[2], ("bk", 2)), (banks[5], ("bk", 5))]
        for n in range(H8):
            for g in range(4):
                for half in range(2):
                    hs = slice(half * 64, half * 64 + 64)
                    sS = sbk_h[half][0][:, 0:128].rearrange("p (g n c) -> p g n c", g=4, n=H8)
                    B.add("pe", OP("matmul", sS[:, g, n, :], lhsT=KnT[hs, n, g, :], rhs=qS[hs, 4 * g:4 * g + 4, n0 + n], start=True, stop=True),
                          reads=["KnT", "qS"], writes=[sbk_h[half][1]])
        for half in range(2):
            B.add("act", OP("activation", out=PTs[:, half * 128:(half + 1) * 128], in_=sbk_h[half][0][:, 0:128], func=AF.Exp, scale=0.125),
                  reads=[sbk_h[half][1]], writes=["PTs"])
        PTv = PTs[:].rearrange("p (h g n c) -> p h g n c", h=2, g=4, n=H8)
        numS = banks[3][:, 0:128].rearrange("p (n c) -> p n c", n=H8)
        denS = banks[4][:, 0:128].rearrange("p (n c) -> p n c", n=H8)
        for n in range(H8):
            for g in range(4):
                for half in range(2):
                    hs = slice(half * 64, half * 64 + 64)
                    B.add("pe", OP("matmul", numS[hs, n, 4 * g:4 * g + 4], lhsT=Vcb[:, n, g * 64:(g + 1) * 64], rhs=PTv[:, half, g, n, :],
                        start=True, stop=True), reads=["PTs", "Vcb"], writes=[("bk", 3)])
                    B.add("pe", OP("matmul", denS[hs, n, 4 * g:4 * g + 4], lhsT=ones_bf[:, 0:64], rhs=PTv[:, half, g, n, :],
                        start=True, stop=True), reads=["PTs", "ones_bf"], writes=[("bk", 4)])
        dS = dtmp[0][:, 0:128].rearrange("p (n c) -> p n c", n=H8)
        B.add("dve", OP("tensor_tensor", out=dS, in0=denS, in1=esink[:, :].unsqueeze(1).broadcast_to([128, H8, 16]), op=ALU.add),
              reads=[("bk", 4), "esink"], writes=[("dtmp", 0)])
        B.add("dve", OP("reciprocal", out=dS, in_=dS), reads=[("dtmp", 0)], writes=[("dtmp", 0)])
        B.add("dve", OP("tensor_tensor", out=attnT[:, :, 1026 + n0:1026 + n0 + H8], in0=numS.rearrange("p n c -> p c n"),
                        in1=dS.rearrange("p n c -> p c n"), op=ALU.mult),
              reads=[("bk", 3), ("dtmp", 0)], writes=attn_res)
    release(m_sa)

    if stop == "1c":
        return finish()
    pooledT = sb("pooledT", [128, 8, NB], BF16)
    m_pool = mark()
    utemps = [sb(f"utemp{i}", [128, 1168], F32) for i in range(2)]
    sA = sb("sA", [128, 1152], F32)
    sBt = sb("sBt", [128, 1152], F32)
    dT = sb("dT", [128, 4, NB], BF16)
    SP = sb("SPst", [120, 2, 1024], F32)
    selT = sb("selT", [120, 4, 8], F32)
    invc = sb("invc", [128, 4, 16], F32)
    tmp16 = sb("tmp16", [128, 16], F32)
    tot16 = sb("tot16", [128, 16], F32)
    uS = sb("uS", [16, 1024], F32)
    uP = sb("uP", [16, 1024], F32)
    B.add("sp", OP("dma_start", out=SP[:], in_=state_pool.rearrange("(a n) h c -> (n h) a c", a=2)), writes=["SP"], dma="sp_l")
    B.add("sp", OP("dma_start", out=selT[:], in_=sel_d), writes=["selT"], dma="sel_l")
    B.add("sp", OP("dma_start", out=invc[:], in_=invc_d), writes=["invc"], dma="invc_l")
    B.add("sp", OP("dma_start", out=pool_s_o[:, 0:14, :], in_=state_pool[:, 1:15, :]), dma="o_pools_a")
    WIN = [2, 4, 8, 16]
    slot_m = take_slot()
    slab_m = wring[slot_m][:, 0:8 * 256].rearrange("p (k c) -> p k c", k=8)
    B.add("pool", OP("dma_start", out=slab_m, in_=w_mix.rearrange("g (kh p) m -> p (g kh) m", p=128)),
          writes=[("w", slot_m)], dma=f"w{slot_m}")
    PIECES_U = [(0, 512), (512, 512), (1026, 16)]

    def pool_small(cc, hb, hres):
        gi = cc // 2
        w = WIN[gi]
        utemp = utemps[cc % 2]
        UT = ("utemp", cc % 2)
        kk = cc % 4
        for a in range(2):
            B.add("pe", OP("matmul", hb[:, 256 + a * 8:256 + a * 8 + 8], lhsT=SP[:, a, cc * 128:(cc + 1) * 128],
                           rhs=selT[:, gi, :], start=True, stop=True), reads=["SP", "selT"], writes=[hres])
        B.add("pe", OP("transpose", hb[0:16, 384:512], utemp[:, 1136:1152], ident32), reads=[UT, "consts32"], writes=[hres])
        B.add("pe", OP("transpose", hb[0:16, 128:256], utemp[:, 1152:1168], ident32), reads=[UT, "consts32"], writes=[hres])
        B.add("dve", OP("tensor_tensor", out=tot16[:], in0=hb[:, 256:272], in1=utemp[:, 1152:1168], op=ALU.add),
              reads=[hres, UT], writes=["tot16"])
        B.add("dve", OP("scalar_tensor_tensor", out=dT[:, kk, 1026:1042], in0=tot16[:], scalar=1.0 / w,
                        in1=utemp[:, 1152:1168], op0=ALU.mult, op1=ALU.subtract), reads=["tot16", UT], writes=[("dT", kk)])
        B.add("act", OP("copy", out=uP[:, cc * 128:(cc + 1) * 128], in_=hb[0:16, 384:512]), reads=[hres], writes=["uP"])
        B.add("act", OP("copy", out=uS[:, cc * 128:(cc + 1) * 128], in_=hb[0:16, 128:256]), reads=[hres], writes=["uS"])

    def pool_mix(gi):
        for oh in range(2):
            grp2 = next_group()
            for kh in range(2):
                ks = (2 * gi + kh) % 4
                for (bank, off, res), (c0, n) in zip(grp2, PIECES_B):
                    B.add("pe", OP("matmul", bank[:, off:off + n], lhsT=slab_m[:, gi * 2 + kh, oh * 128:(oh + 1) * 128],
                                   rhs=dT[:, ks, c0:c0 + n], start=(kh == 0), stop=(kh == 1)),
                          reads=[("w", slot_m), ("dT", ks)], writes=[res])
            oc = 2 * gi + oh
            for (bank, off, res), (c0, n) in zip(grp2, PIECES_B):
                B.add("act", OP("activation", out=pooledT[:, oc, c0:c0 + n], in_=bank[:, off:off + n], func=AF.Copy,
                                scale=colsT[:, C_PSCALE + oc:C_PSCALE + oc + 1]), reads=[res, "colsT"], writes=[("pooledT", oc)])

    for cc in range(8):
        gi = cc // 2
        w = WIN[gi]
        if cc % 4 == 0:
            slot_u, slab_u = load_slab(w_u, KC, 512 * (cc // 4), 512, "wu")
        grp = next_group()
        hbi = 13 - grp[2][2][1]
        hb, hres = banks[hbi], ("bk", hbi)
        mc = (cc % 4) * 128
        utemp = utemps[cc % 2]
        UT = ("utemp", cc % 2)
        for k in range(KC):
            B.add("pe", OP("matmul", hb[:, 0:128], lhsT=slab_u[:, k, mc:mc + 128], rhs=xThb[:, k, 0:128],
                           start=(k == 0), stop=(k == KC - 1)), reads=[("w", slot_u)] + xThb_res, writes=[hres])
            for (bank, off, res), (c0, n) in zip(grp, PIECES_U):
                B.add("pe", OP("matmul", bank[:, off:off + n], lhsT=slab_u[:, k, mc:mc + 128], rhs=xT[:, k, c0:c0 + n],
                               start=(k == 0), stop=(k == KC - 1)), reads=[("w", slot_u)] + xT_res, writes=[res])
        B.add("act", OP("copy", out=utemp[:, 0:128], in_=hb[:, 0:128]), reads=[hres], writes=[UT])
        for (bank, off, res), (d0, n) in zip(grp, [(128, 512), (640, 512), (1152, 16)]):
            B.add("act", OP("copy", out=utemp[:, d0:d0 + n], in_=bank[:, off:off + n]), reads=[res], writes=[UT])
        if cc >= 1:
            pool_small(cc - 1, hb, hres)
            if (cc - 1) % 2 == 1:
                pool_mix((cc - 1) // 2)
        L = 1152
        cur, cur_res, sh = utemp, UT, 1
        bufs = [(sA, "sA"), (sBt, "sBt")]
        bi_ = 0
        lo_ = 0
        while sh < w:
            ob, ores = bufs[bi_ % 2]
            bi_ += 1
            lo_ += sh
            B.add("dve", OP("tensor_tensor", out=ob[:, lo_:L], in0=cur[:, lo_:L], in1=cur[:, lo_ - sh:L - sh], op=ALU.add),
                  reads=[cur_res], writes=[ores])
            cur, cur_res = ob, ores
            sh *= 2
        tot, tot_res = cur, cur_res
        kk = cc % 4
        B.add("dve", OP("scalar_tensor_tensor", out=dT[:, kk, 0:1024], in0=tot[:, 128:1152], scalar=1.0 / w,
                        in1=utemp[:, 128:1152], op0=ALU.mult, op1=ALU.subtract), reads=[tot_res, UT], writes=[("dT", kk)])
        B.add("dve", OP("scalar_tensor_tensor", out=dT[:, kk, 1024:1026], in0=tot[:, 126:128], scalar=1.0 / w,
                        in1=utemp[:, 126:128], op0=ALU.mult, op1=ALU.subtract), reads=[tot_res, UT], writes=[("dT", kk)])
        B.add("dve", OP("tensor_tensor", out=tmp16[:], in0=tot[:, 128:144], in1=invc[:, gi, :], op=ALU.mult),
              reads=[tot_res, "invc"], writes=["tmp16"])
        B.add("dve", OP("tensor_tensor", out=dT[:, kk, 0:16], in0=tmp16[:], in1=utemp[:, 128:144], op=ALU.subtract),
              reads=["tmp16", UT], writes=[("dT", kk)])
    pool_small(7, hb, hres)
    pool_mix(3)
    B.add("sp", OP("dma_start", out=pool_p_o, in_=uP[:]), reads=["uP"], dma="o_poolp")
    B.add("sp", OP("dma_start", out=pool_s_o[:, 14, :], in_=uS[:]), reads=["uS"], dma="o_pools_b")
    pooled_res = [("pooledT", oc) for oc in range(8)]
    release(m_pool)

    if stop == "1d":
        return finish()
    m_1e = mark()
    w4 = sb("wring3", [128, 8192], BF16)
    wring.append(w4)
    wstate["lru"].insert(0, 3)
    sgp = [sb(f"sgp{i}", [128, NB], BF16) for i in range(2)]
    sga = [sb(f"sga{i}", [128, NB], BF16) for i in range(2)]
    m1 = [sb(f"m1_{i}", [128, NB], F32) for i in range(2)]
    m2 = [sb(f"m2_{i}", [128, NB], F32) for i in range(2)]
    mst = [sb(f"mst{i}", [128, NB], BF16) for i in range(2)]
    for j in range(KC):
        if j % 4 == 0:
            slot_gp, slab_gp = load_slab(w_gp, KC, 512 * (j // 4), 512, "wgp")
            slot_ga, slab_ga = load_slab(w_ga, KC, 512 * (j // 4), 512, "wga")
            slot_pb, slab_pb = load_slab(w_pb, 8, 512 * (j // 4), 512, "wpb")
            slot_ab, slab_ab = load_slab(w_ab, KC, 512 * (j // 4), 512, "wab")
        mc = (j % 4) * 128
        i2 = j % 2
        g1 = next_group()
        mm_acc(g1, slot_gp, slab_gp, KC, mc, 128, xT, xT_res)
        for (bank, off, res), (c0, n) in zip(g1, PIECES_B):
            B.add("act", OP("activation", out=sgp[i2][:, c0:c0 + n], in_=bank[:, off:off + n], func=AF.Sigmoid),
                  reads=[res], writes=[("sgp", i2)])
        g2 = next_group()
        mm_acc(g2, slot_ga, slab_ga, KC, mc, 128, xT, xT_res)
        for (bank, off, res), (c0, n) in zip(g2, PIECES_B):
            B.add("act", OP("activation", out=sga[i2][:, c0:c0 + n], in_=bank[:, off:off + n], func=AF.Sigmoid),
                  reads=[res], writes=[("sga", i2)])
        g3 = next_group()
        mm_acc(g3, slot_pb, slab_pb, 8, mc, 128, pooledT, pooled_res)
        for (bank, off, res), (c0, n) in zip(g3, PIECES_B):
            B.add("dve", OP("tensor_tensor", out=m1[i2][:, c0:c0 + n], in0=bank[:, off:off + n],
                                                                                       in1=sgp[i2][:, c0:c0 + n], op=ALU.mult),
                  reads=[res, ("sgp", i2)], writes=[("m1", i2)])
        g4 = next_group()
        mm_acc(g4, slot_ab, slab_ab, KC, mc, 128, attnT, attn_res)
        for (bank, off, res), (c0, n) in zip(g4, PIECES_B):
            B.add("dve", OP("tensor_tensor", out=m2[i2][:, c0:c0 + n], in0=bank[:, off:off + n],
                                                                                       in1=sga[i2][:, c0:c0 + n], op=ALU.mult),
                  reads=[res, ("sga", i2)], writes=[("m2", i2)])
        B.add("dve", OP("tensor_tensor", out=mst[i2][:], in0=m1[i2][:], in1=m2[i2][:], op=ALU.add),
              reads=[("m1", i2), ("m2", i2)], writes=[("mst", i2)])
        B.add("sp", OP("dma_start", out=mg_d[j], in_=mst[i2][:]), reads=[("mst", i2)], writes=[("mg_d", j)], dma=f"mg_st{i2}")
    release(m_1e)
    release(m_phase1)
    wring.pop()
    wstate["lru"].remove(3)

    if stop == "1e":
        return finish()
    m_x1 = mark()
    x1T = sb("x1T", [128, KC, NB], BF16)
    m_p2 = mark()
    yT = sb("yT", [128, KC, NB], F32)
    m_mg = mark()
    mergedT = sb("mergedT", [128, KC, NB], BF16)
    for j in range(KC):
        B.add("sp", OP("dma_start", out=mergedT[:, j, :], in_=mg_d[j]), reads=[("mg_d", j)], writes=[("mergedT", j)], dma=f"mg_ld{j % 4}")
    mg_res = [("mergedT", j) for j in range(KC)]
    for i in range(KC):
        if i % 4 == 0:
            slot_o, slab_o = load_slab(w_o, KC, 512 * (i // 4), 512, "wo")
        grp = next_group()
        mm_acc(grp, slot_o, slab_o, KC, (i % 4) * 128, 128, mergedT, mg_res)
        for pi_, ((bank, off, res), (c0, n)) in enumerate(zip(grp, PIECES_B)):
            if pi_ == 0:
                B.add("act", OP("copy", out=yT[:, i, c0:c0 + n], in_=bank[:, off:off + n]),
                      reads=[res], writes=[("yT", i)])
            else:
                B.add("dve", OP("tensor_copy", out=yT[:, i, c0:c0 + n], in_=bank[:, off:off + n]),
                      reads=[res], writes=[("yT", i)])
    yT_res = [("yT", i) for i in range(KC)]
    release(m_mg)

    if stop == "1f":
        return finish()
    m_ln = mark()
    grep_ = sb("g_rep", [128, D], F32)
    brep_ = sb("b_rep", [128, D], F32)
    xres = [sb(f"xres{i}", [128, D], F32) for i in range(2)]
    ytok = [sb(f"ytok{i}", [128, D], F32) for i in range(2)]
    stats = sb("stats", [128, 4, 6], F32)
    mv = sb("mv", [128, 2], F32)
    rstd = sb("rstd", [128, 1], F32)
    nmr = sb("nmr", [128, 1], F32)
    B.add("sp", OP("dma_start", out=grep_[:], in_=rep_d[0]), writes=["g_rep"], dma="rep_g")
    B.add("sp", OP("dma_start", out=brep_[:], in_=rep_d[1]), writes=["b_rep"], dma="rep_b")

    def layernorm_tile(nt, src_banks, src_res, xr, xr_res, yt, yt_res, from_psum):
        for qd in range(4):
            cs = slice(qd * 512, qd * 512 + 512)
            if from_psum is None:
                pass
            elif from_psum:
                bank, bres = src_banks[qd]
                B.add("dve", OP("scalar_tensor_tensor", out=yt[0:nt, cs], in0=xr[0:nt, cs], scalar=ALPHA,
                                                                                 in1=bank[0:nt, :], op0=ALU.mult, op1=ALU.add),
                      reads=[bres, xr_res], writes=[yt_res, (yt_res, "a"), (yt_res, "b")])
            else:
                B.add("dve", OP("scalar_tensor_tensor", out=yt[0:nt, cs], in0=xr[0:nt, cs], scalar=ALPHA,
                                                                      in1=src_banks[0:nt, cs], op0=ALU.mult, op1=ALU.add),
                      reads=[src_res, xr_res], writes=[yt_res, (yt_res, "a"), (yt_res, "b")])
            B.add("dve", OP("bn_stats", out=stats[0:nt, qd, :], in_=yt[0:nt, cs]), reads=[yt_res], writes=["stats"])
        B.add("dve", OP("bn_aggr", out=mv[0:nt, :], in_=stats[0:nt, :, :].rearrange("p a b -> p (a b)")), reads=["stats"], writes=["mv"])
        B.add("act", OP("activation", out=rstd[0:nt, :], in_=mv[0:nt, 1:2], func=AF.Sqrt, bias=colsT[0:nt, C_EPS:C_EPS + 1], scale=1.0),
              reads=["mv", "colsT"], writes=["rstd"])
        B.add("dve", OP("reciprocal", out=rstd[0:nt, :], in_=rstd[0:nt, :]), reads=["rstd"], writes=["rstd"])
        B.add("dve", OP("scalar_tensor_tensor", out=nmr[0:nt, :], in0=mv[0:nt, 0:1], scalar=-1.0, in1=rstd[0:nt, :], op0=ALU.mult, op1=ALU.mult),
              reads=["mv", "rstd"], writes=["nmr"])
        B.add("act", OP("activation", out=yt[0:nt, :], in_=yt[0:nt, :], func=AF.Identity, bias=nmr[0:nt, :], scale=rstd[0:nt, :]),
              reads=[yt_res, "nmr", "rstd"], writes=[yt_res])
        ya, yb = (yt_res, "a"), (yt_res, "b")
        B.add("pool", OP("tensor_tensor", out=yt[0:nt, 0:1024], in0=yt[0:nt, 0:1024], in1=grep_[0:nt, 0:1024], op=ALU.mult), reads=[yt_res, "g_rep"], writes=[ya])
        B.add("dve", OP("tensor_tensor", out=yt[0:nt, 1024:2048], in0=yt[0:nt, 1024:2048], in1=grep_[0:nt, 1024:2048], op=ALU.mult), reads=[yt_res, "g_rep"], writes=[yb])
        B.add("pool", OP("tensor_tensor", out=yt[0:nt, 0:1024], in0=yt[0:nt, 0:1024], in1=brep_[0:nt, 0:1024], op=ALU.add), reads=[ya, "b_rep"], writes=[ya])
        B.add("dve", OP("tensor_tensor", out=yt[0:nt, 1024:2048], in0=yt[0:nt, 1024:2048], in1=brep_[0:nt, 1024:2048], op=ALU.add), reads=[yb, "b_rep"], writes=[yb])

    ffn_seq = []
    for s_ in range(11):
        ffn_seq.append((w_gate, s_))
        ffn_seq.append((w_up, s_))
    ffn_loaded = []

    def ffn_ensure(upto):
        while len(ffn_loaded) <= min(upto, len(ffn_seq) - 1):
            W_, s_ = ffn_seq[len(ffn_loaded)]
            ffn_loaded.append(load_slab(W_, KC, 512 * s_, 512, "wffn"))

    ffn_ensure(2)
    ttiles = [(128 * i, 128) for i in range(8)] + [(1024, 18)]
    pend_post = None
    for ti, (c0, nt) in enumerate(ttiles):
        xr, xr_res = xres[ti % 2], ("xres", ti % 2)
        yt, yt_res = ytok[ti % 2], ("ytok", ti % 2)
        if nt == 128:
            B.add("sp", OP("dma_start", out=xr[:, :], in_=xin[256 + c0:256 + c0 + 128, :]), writes=[xr_res], dma=f"xres{ti % 2}")
        else:
            B.add("sp", OP("dma_start", out=xr[0:2, :], in_=xin[254:256, :]), writes=[xr_res], dma=f"xres{ti % 2}")
            B.add("sp", OP("dma_start", out=xr[2:18, :], in_=xin[1280:1296, :]), writes=[xr_res], dma=f"xres{ti % 2}b")
        sbk = [(banks[q], ("bk", q)) for q in range(4)]
        for qd in range(4):
            for jj in range(4):
                kc = qd * 4 + jj
                B.add("pe", OP("transpose", banks[qd][0:nt, jj * 128:(jj + 1) * 128],
                                                                                       yT[:, kc, c0:c0 + nt], ident32),
                      reads=[("yT", kc), "consts32"], writes=[("bk", qd)])
        layernorm_tile(nt, sbk, None, xr, xr_res, yt, yt_res, True)
        def post(yt=yt, yt_res=yt_res, c0=c0, nt=nt, ti=ti):
            YR = [yt_res, (yt_res, "a"), (yt_res, "b")]
            B.add("act", OP("dma_start", out=x1_d[c0:c0 + nt, :], in_=yt[0:nt, :]), reads=YR, writes=[("x1_d", ti)], dma=f"x1st{ti % 2}")
            for qd in range(4):
                bq = 4 + qd
                bres = ("bk", bq)
                extra = []
                for jj in range(4):
                    kc = qd * 4 + jj
                    B.add("pe", OP("transpose", banks[bq][:, jj * 128:jj * 128 + nt],
                                                                                          yt[0:nt, kc * 128:(kc + 1) * 128], ident32[0:nt, 0:nt]),
                          reads=YR + ["consts32"], writes=[bres] + extra)
                srcv = banks[bq][:, :].rearrange("p (j c) -> p j c", j=4)[:, :, 0:nt]
                B.add("act", OP("copy", out=x1T[:, qd * 4:qd * 4 + 4, c0:c0 + nt], in_=srcv),
                      reads=[bres], writes=[("x1T", qd)])
        if pend_post is not None:
            pend_post()
        pend_post = post
    pend_post()
    x1T_res = [("x1T", q) for q in range(4)]
    release(m_ln)
    release(m_p2)

    if stop == "2a":
        return finish()
    hT = sb("hT", [128, FC, 1040], BF16)
    m_ffn = mark()
    histT = sb("histT", [128, FC, 32], F32)
    gcol = sb("gcol", [128, FC, 18], F32)
    m_sc = mark()
    SC = sb("SC", [32, DFF], F32)
    B.add("sp", OP("dma_start", out=SC[:], in_=state_conv.rearrange("n h f -> (n h) f")), writes=["SC"], dma="sc_l")
    B.add("sp", OP("dma_start", out=conv_s_o[:, 0, :], in_=state_conv[:, 1, :]), dma="o_convs_a")
    for c in range(FC):
        bq = c % 4
        B.add("pe", OP("transpose", banks[bq][:, 0:32], SC[:, c * 128:(c + 1) * 128], ident32[0:32, 0:32]),
              reads=["SC", "consts32"], writes=[("bk", bq)])
        B.add("act", OP("copy", out=histT[:, c, :], in_=banks[bq][:, 0:32]), reads=[("bk", bq)], writes=["histT"])
    release(m_sc)
    m_gs = mark()
    gs = [sb(f"gs{i}", [128, NB], F32) for i in range(2)]
    cv = [sb(f"cv{i}", [128, 1040], F32) for i in range(1)]
    ge = [sb(f"ge{i}", [128, 1040], BF16) for i in range(1)]
    for c in range(FC):
        if c % 4 == 0:
            ffn_ensure(2 * (c // 4) + 2)
            slot_g, slab_g = ffn_loaded[2 * (c // 4)]
            slot_up, slab_up = ffn_loaded[2 * (c // 4) + 1]
        mc = (c % 4) * 128
        i2 = c % 2
        gsb, gres = gs[i2], ("gs", i2)
        cvb, cres = cv[0], ("cv", 0)
        geb, geres = ge[0], ("ge", 0)
        gg = next_group()
        mm_acc(gg, slot_g, slab_g, KC, mc, 128, x1T, x1T_res)
        (b0, o0, r0), (b1, o1, r1), (b2, o2, r2) = gg
        B.add("act", OP("copy", out=gsb[:, 2:514], in_=b0[:, 0:512]), reads=[r0], writes=[gres])
        B.add("act", OP("copy", out=gsb[:, 514:1026], in_=b1[:, 0:512]), reads=[r1], writes=[gres])
        B.add("act", OP("activation", out=gsb[:, 0:2], in_=b2[:, o2:o2 + 2], func=AF.Copy,
                                                                    scale=colsT[:, C_HV:C_HV + 1]), reads=[r2, "colsT"], writes=[gres])
        B.add("act", OP("copy", out=gsb[:, 1026:1042], in_=b2[:, o2 + 2:o2 + 18]), reads=[r2], writes=[gres])
        gu = next_group()
        mm_acc(gu, slot_up, slab_up, KC, mc, 128, x1T, x1T_res)
        w0 = colsT[:, C_CONVW + c:C_CONVW + c + 1]
        w1 = colsT[:, C_CONVW + FC + c:C_CONVW + FC + c + 1]
        w2 = colsT[:, C_CONVW + 2 * FC + c:C_CONVW + 2 * FC + c + 1]
        cb = colsT[:, C_CONVB + c:C_CONVB + c + 1]
        B.add("dve", OP("tensor_scalar", out=cvb[:, 0:1024], in0=gsb[:, 2:1026], scalar1=w2, scalar2=cb,
                                                                                op0=ALU.mult, op1=ALU.add), reads=[gres, "colsT"], writes=[cres])
        B.add("dve", OP("scalar_tensor_tensor", out=cvb[:, 0:1024], in0=gsb[:, 1:1025], scalar=w1, in1=cvb[:, 0:1024],
                                                                                op0=ALU.mult, op1=ALU.add), reads=[gres, cres, "colsT"], writes=[cres])
        B.add("dve", OP("scalar_tensor_tensor", out=cvb[:, 0:1024], in0=gsb[:, 0:1024], scalar=w0, in1=cvb[:, 0:1024],
                                                                                op0=ALU.mult, op1=ALU.add), reads=[gres, cres, "colsT"], writes=[cres])
        hv = histT[:, c, :].rearrange("p (n h) -> p n h", h=2)
        B.add("dve", OP("tensor_scalar", out=cvb[:, 1024:1040], in0=gsb[:, 1026:1042], scalar1=w2, scalar2=cb,
                                                                                op0=ALU.mult, op1=ALU.add), reads=[gres, "colsT"], writes=[cres])
        B.add("dve", OP("scalar_tensor_tensor", out=cvb[:, 1024:1040], in0=hv[:, :, 1], scalar=w1, in1=cvb[:, 1024:1040],
                                                                              op0=ALU.mult, op1=ALU.add), reads=["histT", cres, "colsT"], writes=[cres])
        B.add("dve", OP("scalar_tensor_tensor", out=cvb[:, 1024:1040], in0=hv[:, :, 0], scalar=w0, in1=cvb[:, 1024:1040],
                                                                              op0=ALU.mult, op1=ALU.add), reads=["histT", cres, "colsT"], writes=[cres])
        B.add("act", OP("activation", out=geb[:, :], in_=cvb[:, :], func=AF.Gelu_apprx_tanh), reads=[cres], writes=[geres])
        B.add("act", OP("copy", out=gcol[:, c, 0:2], in_=gsb[:, 1024:1026]), reads=[gres], writes=["gcol"])
        B.add("act", OP("copy", out=gcol[:, c, 2:18], in_=gsb[:, 1026:1042]), reads=[gres], writes=["gcol"])
        (u0, uo0, ur0), (u1, uo1, ur1), (u2, uo2, ur2) = gu
        B.add("dve", OP("tensor_tensor", out=hT[:, c, 0:512], in0=u0[:, 0:512], in1=geb[:, 0:512], op=ALU.mult),
              reads=[ur0, geres], writes=[("hT", c)])
        B.add("dve", OP("tensor_tensor", out=hT[:, c, 512:1024], in0=u1[:, 0:512], in1=geb[:, 512:1024], op=ALU.mult),
              reads=[ur1, geres], writes=[("hT", c)])
        B.add("dve", OP("tensor_tensor", out=hT[:, c, 1024:1040], in0=u2[:, uo2 + 2:uo2 + 18], in1=geb[:, 1024:1040], op=ALU.mult),
              reads=[ur2, geres], writes=[("hT", c)])
    release(m_gs)
    gst = sb("gst", [18, DFF], F32)
    for c in range(FC):
        bq = c % 4
        B.add("pe", OP("transpose", banks[bq][0:18, 0:128], gcol[:, c, :], ident32), reads=["gcol", "consts32"], writes=[("bk", bq)])
        B.add("act", OP("copy", out=gst[:, c * 128:(c + 1) * 128], in_=banks[bq][0:18, 0:128]), reads=[("bk", bq)], writes=["gst"])
    B.add("sp", OP("dma_start", out=conv_p_o, in_=gst[0:2, :]), reads=["gst"], dma="o_convp")
    B.add("sp", OP("dma_start", out=conv_s_o[:, 1, :], in_=gst[2:18, :]), reads=["gst"], dma="o_convs_b")
    hT_res = [("hT", c) for c in range(FC)]
    release(m_ffn)

    if stop == "2b":
        return finish()
    m_dn = mark()
    ystg = [sb(f"ystg{i}", [128, 128], F32) for i in range(4)]
    x1blk = [sb(f"x1blk{i}", [128, 128], F32) for i in range(8)]
    dtiles = [(128 * i, 128) for i in range(8)] + [(1024, 16)]
    nst = 0
    for jg in range(KC):
        slot_d, slab_d = load_slab(w_down, FC, 128 * jg, 128, "wd")
        for ti, (c0, nt) in enumerate(dtiles):
            bq = nst % 8
            bres = ("bk", bq)
            extra = []
            for c in range(FC):
                B.add("pe", OP("matmul", banks[bq][0:nt, 0:128], lhsT=hT[:, c, c0:c0 + nt],
                                                                                         rhs=slab_d[:, c, 0:128], start=(c == 0), stop=(c == FC - 1)),
                      reads=[("w", slot_d), ("hT", c)] if c in (0, FC - 1) else [("w", slot_d)], writes=[bres] + extra)
            si = nst % 4
            xb = nst % 8
            x1row = c0 if nt == 128 else 1026
            x1tile = ti if nt == 128 else 8
            B.add("sp", OP("dma_start", out=x1blk[xb][0:nt, :], in_=x1_d[x1row:x1row + nt, jg * 128:(jg + 1) * 128]),
                  reads=[("x1_d", x1tile)], writes=[("x1blk", xb)], dma=f"x1blk{xb}")
            B.add("dve", OP("scalar_tensor_tensor", out=ystg[si][0:nt, :], in0=x1blk[xb][0:nt, :], scalar=ALPHA,
                            in1=banks[bq][0:nt, 0:128], op0=ALU.mult, op1=ALU.add),
                  reads=[bres, ("x1blk", xb)], writes=[("ystg", si)])
            B.add("sp", OP("dma_start", out=y2_d[c0:c0 + nt, jg * 128:(jg + 1) * 128], in_=ystg[si][0:nt, :]),
                  reads=[("ystg", si)], writes=[("y2_d", ti)], dma=f"y2st{si}")
            nst += 1
    release(m_dn)

    if stop == "2c":
        return finish()
    release(m_x1)
    m_l2 = mark()
    grep_ = sb("g_rep2", [128, D], F32)
    brep_ = sb("b_rep2", [128, D], F32)
    y2r = [sb(f"y2r{i}", [128, D], F32) for i in range(4)]
    stats = sb("stats2", [128, 4, 6], F32)
    mv = sb("mv2", [128, 2], F32)
    rstd = sb("rstd2", [128, 1], F32)
    nmr = sb("nmr2", [128, 1], F32)
    B.add("sp", OP("dma_start", out=grep_[:], in_=rep_d[2]), writes=["g_rep"], dma="rep_g")
    B.add("sp", OP("dma_start", out=brep_[:], in_=rep_d[3]), writes=["b_rep"], dma="rep_b")
    pend_store = []
    for ti, (c0, nt) in enumerate(dtiles):
        y2, y2_res = y2r[ti % 4], ("y2r", ti % 4)
        B.add("sp", OP("dma_start", out=y2[0:nt, :], in_=y2_d[c0:c0 + nt, :]),
              reads=[("y2_d", ti)], writes=[y2_res, (y2_res, "a"), (y2_res, "b")], dma=f"y2ld{ti % 4}")
        layernorm_tile(nt, None, None, None, None, y2, y2_res, None)
        if len(pend_store) >= 2:
            ps_ = pend_store.pop(0)
            B.add("act", *ps_[0], **ps_[1])
        pend_store.append(((OP("dma_start", out=y_o[c0:c0 + nt, :], in_=y2[0:nt, :]),),
                           dict(reads=[y2_res, (y2_res, "a"), (y2_res, "b")], dma=f"o_y{ti % 4}")))
    for ps_ in pend_store:
        B.add("act", *ps_[0], **ps_[1])
    release(m_l2)

    return finish()


def _emit_program(nc, B):
    B.plan()
    sem_cms = {}
    keys = sorted(B.final.keys())
    for k in keys:
        cm = nc.semaphore("s_" + "_".join(str(x) for x in k))
        sem_cms[k] = cm
    sems = {k: cm.__enter__() for k, cm in sem_cms.items()}
    with nc.Block() as block:
        @block.sync
        def _(e):
            B.emit({"sp": e}, sems)

        @block.gpsimd
        def _(e):
            B.emit({"pool": e}, sems)

        @block.tensor
        def _(e):
            B.emit({"pe": e}, sems)

        @block.scalar
        def _(e):
            B.emit({"act": e}, sems)

        @block.vector
        def _(e):
            B.emit({"dve": e}, sems)
    for cm in sem_cms.values():
        cm.__exit__(None, None, None)
    return nc


def _host_tables(core):
    f32 = np.float32
    half = 32
    inv = 10000.0 ** (-np.arange(half, dtype=np.float64) / half)
    posA = np.concatenate([1024 * core - 256 + np.arange(256), 1024 * core + np.arange(1024), np.full(16, 8192)]).astype(np.int64)
    posA = np.maximum(posA, 0).astype(np.float64)
    p = np.arange(128)
    fidx = p % 32
    sign = np.where((p % 64) < 32, -1.0, 1.0)
    ang = posA[None, :] * inv[fidx][:, None]
    cosA = np.cos(ang).astype(f32)
    sinA = (np.sin(ang) * sign[:, None]).astype(f32)
    cosS = np.concatenate([cosA[:, 254:256], cosA[:, 1280:1296]], axis=1)
    sinS = np.concatenate([sinA[:, 254:256], sinA[:, 1280:1296]], axis=1)
    k = np.arange(128)[:, None]
    q = np.arange(128)[None, :]
    m_cur = (k <= q).astype(f32)
    m_prev = (k > q).astype(f32)
    m_prev0 = m_prev if core > 0 else np.zeros_like(m_prev)
    masks = np.stack([m_cur, m_prev, m_prev0], axis=1)
    sel = np.zeros((120, 4, 8), f32)
    for gi, w in enumerate((2, 4, 8, 16)):
        for n in range(8):
            for h in range(15):
                if h >= 16 - w:
                    sel[n * 15 + h, gi, n] = 1.0
    invc = np.zeros((128, 4, 16), f32)
    for gi, w in enumerate((2, 4, 8, 16)):
        for t in range(16):
            cnt = min(w, 1024 * core + t + 1)
            invc[:, gi, t] = 1.0 / cnt
    return cosA, sinA, cosS, sinS, masks, sel, invc


_PROG = {}


def _prep(x_prompt, x_sample, cache_k, cache_v, state_pool, state_conv,
           w_in, attn_sinks, w_pool_mix, pool_scale, w_attn_branch, w_pool_branch, w_out,
           ln1_g, ln1_b, w_up, w_gate, conv_w, conv_b, w_down, ln2_g, ln2_b):
    f32 = np.float32
    A = lambda a: np.ascontiguousarray(np.asarray(a, dtype=f32))
    x_prompt, x_sample = A(x_prompt), A(x_sample)
    w_in = A(w_in)[0]
    wq = A(w_in[:, 0:2048])
    wk = w_in[:, 2048:2304].reshape(D, 4, 1, 64)
    wkd = A(np.broadcast_to(wk, (D, 4, 2, 64)).reshape(D, 512))
    wv = A(w_in[:, 2304:2560])
    wu = A(w_in[:, 2560:3584])
    wgp = A(w_in[:, 3584:5632])
    wga = A(w_in[:, 5632:7680])
    shared = dict(
        w_q=wq, w_kd=wkd, w_v=wv, w_u=wu, w_gp=wgp, w_ga=wga, w_mix=A(w_pool_mix)[0], w_ab=A(w_attn_branch)[0],
        w_pb=A(w_pool_branch)[0], w_o=A(w_out)[0], w_up=A(w_up)[0], w_gate=A(w_gate)[0], w_down=A(w_down)[0],
    )
    rep = A(np.stack([np.broadcast_to(A(v)[0][None, :], (128, D)) for v in (ln1_g, ln1_b, ln2_g, ln2_b)]))
    consts = np.zeros((128, 3, 128), f32)
    consts[:, 0, :] = np.eye(128, dtype=f32)
    pidx = np.arange(128)
    consts[pidx ^ 32, 1, pidx] = 1.0
    consts[:, 2, :] = 1.0
    p = np.arange(128)
    cols_base = np.zeros((128, 256), f32)
    cols_base[:, 0:8] = A(pool_scale)[0].reshape(8, 128).T
    sk = A(attn_sinks)[0]
    for ch in range(16):
        cols_base[:, 8 + ch] = np.where(p >= 64, sk[2 * ch + 1], sk[2 * ch])
    cw = A(conv_w)[0]
    for j in range(3):
        cols_base[:, 24 + j * FC:24 + (j + 1) * FC] = cw[j].reshape(FC, 128).T
    cols_base[:, 156:156 + FC] = A(conv_b)[0].reshape(FC, 128).T
    cols_base[:, 201] = EPS
    xp = x_prompt[0]
    xs = x_sample[:, 0, :]
    ck, cv_ = A(cache_k)[0].reshape(128, 128, 256), A(cache_v)[0].reshape(128, 128, 256)
    spool, sconv = A(state_pool)[0], A(state_conv)[0]
    in_maps = []
    for c in range(NCORES):
        halo = xp[1024 * c - 256:1024 * c] if c > 0 else np.zeros((256, D), f32)
        xin = A(np.concatenate([halo, xp[1024 * c:1024 * (c + 1)], xs[16 * c:16 * (c + 1)]], axis=0))
        cosA, sinA, cosS, sinS, masks, sel, invc = _host_tables(c)
        cols = cols_base.copy()
        cols[:, 200] = 1.0 if c > 0 else 0.0
        m = dict(shared)
        m.update(xin=xin, cache_k=A(ck[16 * c:16 * (c + 1)]), cache_v=A(cv_[16 * c:16 * (c + 1)]),
                 state_pool=A(spool[16 * c:16 * (c + 1)]), state_conv=A(sconv[16 * c:16 * (c + 1)]),
                 cosA=A(cosA), sinA=A(sinA), cosS=A(cosS), sinS=A(sinS), masks=A(masks), sel=sel, invc=invc,
                 cols=cols, rep=rep, consts=consts)
        in_maps.append(m)
    return in_maps


def kernel(**inputs):
    in_maps = _prep(**inputs)
    if "nc" not in _PROG:
        _PROG["nc"] = build_program()
    res = run_bass_kernel_spmd(_PROG["nc"], in_maps, core_ids=list(range(NCORES)))
    return _assemble(res.results)


def _assemble(R):
    y_prompt = np.concatenate([R[c]["y"][0:1024] for c in range(NCORES)], axis=0)[None]
    y_sample = np.concatenate([R[c]["y"][1024:1040] for c in range(NCORES)], axis=0)[:, None, :]
    kc_p = R[7]["kc_p"].reshape(1, 1, 128, 4, 64)
    vc_p = R[7]["vc_p"].reshape(1, 1, 128, 4, 64)
    pool_p = R[7]["pool_p"][1:16].reshape(1, 1, 15, 1024)
    conv_p = R[7]["conv_p"].reshape(1, 1, 2, DFF)
    kc_s = np.concatenate([R[c]["kc_s"] for c in range(NCORES)], axis=0).reshape(1, 128, 128, 4, 64)
    vc_s = np.concatenate([R[c]["vc_s"] for c in range(NCORES)], axis=0).reshape(1, 128, 128, 4, 64)
    pool_s = np.concatenate([R[c]["pool_s"] for c in range(NCORES)], axis=0).reshape(1, 128, 15, 1024)
    conv_s = np.concatenate([R[c]["conv_s"] for c in range(NCORES)], axis=0).reshape(1, 128, 2, DFF)
    outs = (y_prompt, y_sample, kc_p, vc_p, pool_p, conv_p, kc_s, vc_s, pool_s, conv_s)
    return tuple(np.ascontiguousarray(o, dtype=np.float32) for o in outs)
```

```python
import numpy as np
import concourse.bass as bass
import concourse.mybir as mybir
from concourse.bass_utils import run_bass_kernel_spmd

F32 = mybir.dt.float32
BF16 = mybir.dt.bfloat16
AF = mybir.ActivationFunctionType
ALU = mybir.AluOpType

NCORES = 8
D = 2048
KC = 16
T = 1024
NS = 16
NB = 1042
NA = 1296
DFF = 5632
FC = 44
ALPHA = 2.0 ** 0.25
EPS = 1e-5
PIECES_B = [(0, 512), (512, 512), (1024, 18)]


def OP(meth, *a, **kw):
    return (meth, a, kw)


class Builder:
    def __init__(self, nc):
        self.nc = nc
        self.ops = []
        self.last_write = {}
        self.readers = {}
        self.last_sk = {}
        self.pending = {}

    def fence(self):
        f = set(self.last_sk.values())
        for q in ("pe", "act", "dve", "pool", "sp"):
            self.pending[q] = set(f) | self.pending.get(q, set())

    def add(self, q, fn, reads=(), writes=(), dma=None, nofence=False):
        if q == "pe":
            meth, a, kw = fn
            opnd = kw["lhsT"] if meth == "matmul" else a[1]
            if opnd.dtype == F32:
                self.pe_f32 = True
            else:
                if getattr(self, "pe_f32", False) and opnd.shape[-1] == 128 and len(opnd.shape) == 2:
                    out = a[0]
                    assert len(out.shape) == 2 and (meth != "matmul" or kw.get("start", True)), (meth, out.shape)
                    self.pe_f32 = False
                    self.add("pe", OP("matmul", out[0:32, 0:1], lhsT=self.dummy_w[:, 0:32], rhs=self.dummy_w[:, 0:1],
                                      start=True, stop=True), reads=list(reads), writes=list(writes))
                self.pe_f32 = False
        idx = len(self.ops)
        deps = set()
        for r in reads:
            if r in self.last_write:
                deps.add(self.last_write[r])
            if isinstance(r, tuple) and r[0] == "bk":
                for sk2, rd in self.readers.get(r, {}).items():
                    if sk2 != ("dma", dma) and sk2 != ("eng", q):
                        deps.add(rd)
        for w in writes:
            if w in self.last_write:
                deps.add(self.last_write[w])
            for rd in self.readers.get(w, {}).values():
                deps.add(rd)
        sk = ("dma", dma) if dma is not None else ("eng", q)
        if self.pending.get(q) and not nofence:
            deps |= self.pending[q]
            self.pending[q] = set()
        self.last_sk[sk] = idx
        for r in reads:
            self.readers.setdefault(r, {})[sk] = idx
        for w in writes:
            self.last_write[w] = idx
            self.readers[w] = {}
        deps.discard(idx)
        self.ops.append(dict(q=q, fn=fn, deps=deps, dma=dma, sk=sk, signal=dma is not None, val=None))
        return idx

    def plan(self):
        ops = self.ops
        for op in ops:
            need = {}
            for d in op["deps"]:
                dop = ops[d]
                if dop["sk"] == ("eng", "pe") and op["sk"] == ("eng", "pe"):
                    continue
                k = dop["sk"]
                if k not in need or need[k] < d:
                    need[k] = d
            op["need"] = need
            for d in need.values():
                ops[d]["signal"] = True
        cnt = {}
        for op in ops:
            if op["signal"]:
                inc = 16 if op["dma"] is not None else 1
                cnt[op["sk"]] = cnt.get(op["sk"], 0) + inc
                op["val"] = cnt[op["sk"]]
        self.final = cnt

    def emit(self, block_engines, sems):
        ops = self.ops
        for q, eng in block_engines.items():
            waited = {}
            for op in ops:
                if op["q"] != q:
                    continue
                for k, d in op["need"].items():
                    v = ops[d]["val"]
                    if waited.get(k, 0) >= v:
                        continue
                    eng.wait_ge(sems[k], v)
                    waited[k] = v
                meth, a, kw = op["fn"]
                inst = getattr(eng, meth)(*a, **kw)
                if op["signal"]:
                    inst.then_inc(sems[op["sk"]], 16 if op["dma"] is not None else 1)
            if q == "sp":
                for k, v in self.final.items():
                    if k[0] == "dma":
                        eng.wait_ge(sems[k], v)


def build_program(stop=None):
    nc = bass.Bass("TRN2", target_bir_lowering=False)
    B = Builder(nc)

    def finish():
        _emit_program(nc, B)
        release(0)
        return nc

    def din(name, shape, dt=F32):
        return nc.dram_tensor(name, list(shape), dt, kind="ExternalInput").ap()

    def dout(name, shape):
        return nc.dram_tensor(name, list(shape), F32, kind="ExternalOutput").ap()

    xin = din("xin", [NA, D])
    cache_k = din("cache_k", [NS, 128, 256])
    cache_v = din("cache_v", [NS, 128, 256])
    state_pool = din("state_pool", [NS, 15, 1024])
    state_conv = din("state_conv", [NS, 2, DFF])
    w_q = din("w_q", [D, 2048])
    w_kd = din("w_kd", [D, 512])
    w_v = din("w_v", [D, 256])
    w_u = din("w_u", [D, 1024])
    w_gp = din("w_gp", [D, 2048])
    w_ga = din("w_ga", [D, 2048])
    w_mix = din("w_mix", [4, 256, 256])
    w_ab = din("w_ab", [2048, 2048])
    w_pb = din("w_pb", [1024, 2048])
    w_o = din("w_o", [2048, 2048])
    w_up = din("w_up", [D, DFF])
    w_gate = din("w_gate", [D, DFF])
    w_down = din("w_down", [DFF, D])
    cosA_d = din("cosA", [128, NA])
    sinA_d = din("sinA", [128, NA])
    cosS_d = din("cosS", [128, 18])
    sinS_d = din("sinS", [128, 18])
    masks_d = din("masks", [128, 3, 128])
    sel_d = din("sel", [120, 4, 8])
    invc_d = din("invc", [128, 4, 16])
    cols_d = din("cols", [128, 256])
    rep_d = din("rep", [4, 128, D])
    consts_d = din("consts", [128, 3, 128])

    y_o = dout("y", [T + NS, D])
    kc_p_o = dout("kc_p", [128, 256])
    vc_p_o = dout("vc_p", [128, 256])
    pool_p_o = dout("pool_p", [16, 1024])
    conv_p_o = dout("conv_p", [2, DFF])
    kc_s_o = dout("kc_s", [NS, 128, 256])
    vc_s_o = dout("vc_s", [NS, 128, 256])
    pool_s_o = dout("pool_s", [NS, 15, 1024])
    conv_s_o = dout("conv_s", [NS, 2, DFF])

    x1_d = nc.dram_tensor("x1_scr", [NB, D], F32).ap()
    y2_d = nc.dram_tensor("y2_scr", [T + NS, D], F32).ap()
    kv_new_d = nc.dram_tensor("kvnew_scr", [2, NS, 256], F32).ap()
    mg_d = nc.dram_tensor("mg_scr", [KC, 128, NB], BF16).ap()

    C_PSCALE = 0
    C_SINK = 8
    C_CONVW = 24
    C_CONVB = 156
    C_HV = 200
    C_EPS = 201

    stack = []

    uniq = dict(n=0)

    def sb(name, shape, dt):
        uniq["n"] += 1
        cm = nc.sbuf_tensor(f"sb{uniq['n']}_{name}", list(shape), dt)
        t = cm.__enter__()
        stack.append(cm)
        return t

    def mark():
        return len(stack)

    def release(m):
        if len(stack) > m:
            B.fence()
        while len(stack) > m:
            stack.pop().__exit__(None, None, None)

    def ps(name):
        cm = nc.psum_tensor(name, [128, 512], F32)
        t = cm.__enter__()
        stack.append(cm)
        return t

    banks = [ps(f"bank{i}") for i in range(8)]

    consts32 = sb("consts32", [128, 3, 128], F32)
    ident_bf = sb("ident_bf", [128, 128], BF16)
    perm_bf = sb("perm_bf", [128, 128], BF16)
    ones_bf = sb("ones_bf", [128, 128], BF16)
    colsT = sb("colsT", [128, 256], F32)
    esink = sb("esink", [128, 16], F32)
    wring = [sb(f"wring{i}", [128, 8192], BF16) for i in range(3)]
    ident32 = consts32[:, 0, :]
    B.dummy_w = ident_bf

    wstate = dict(n=0, lru=[0, 1, 2])

    def take_slot():
        s = wstate["lru"].pop(0)
        wstate["lru"].append(s)
        return s

    def load_slab(W, row_kc, c0, cw, name):
        slot = take_slot()
        dst = wring[slot][:, 0:row_kc * cw].rearrange("p (k c) -> p k c", k=row_kc)
        src = W[:, c0:c0 + cw].rearrange("(k p) c -> p k c", p=128)
        B.add("pool", OP("dma_start", out=dst, in_=src),
              reads=[], writes=[("w", slot)], dma=f"w{slot}", nofence=(slot < 3))
        return slot, dst

    accn = dict(n=0)

    def next_group():
        gi = accn["n"] % 3
        sm = 6 + (accn["n"] % 2)
        accn["n"] += 1
        return [(banks[2 * gi], 0, ("bk", 2 * gi)), (banks[2 * gi + 1], 0, ("bk", 2 * gi + 1)),
                (banks[sm], 0, ("bk", sm))]

    def mm_acc(grp, slot, slab, kcs, mcol, msz, src, src_res, pieces=PIECES_B, src_col=lambda c: c):
        for sel in (grp[0:2], grp[2:3]):
            psel = pieces[0:2] if sel is not grp[2:3] and len(sel) == 2 else pieces[2:3]
            for k in range(kcs):
                for (bank, off, res), (c0, n) in zip(sel, psel):
                    B.add("pe", OP("matmul",
                        bank[0:msz, off:off + n], lhsT=slab[:, k, mcol:mcol + msz], rhs=src[:, k, src_col(c0):src_col(c0) + n],
                        start=(k == 0), stop=(k == kcs - 1)),
                        reads=[("w", slot)] + src_res, writes=[res])

    B.add("sp", OP("dma_start", out=consts32[:], in_=consts_d), writes=["consts32"], dma="c0")
    B.add("sp", OP("dma_start", out=colsT[:], in_=cols_d), writes=["colsT"], dma="c1")
    B.add("dve", OP("tensor_copy", out=ident_bf[:], in_=consts32[:, 0, :]), reads=["consts32"], writes=["ident_bf"])
    B.add("dve", OP("tensor_copy", out=perm_bf[:], in_=consts32[:, 1, :]), reads=["consts32"], writes=["perm_bf"])
    B.add("dve", OP("tensor_copy", out=ones_bf[:], in_=consts32[:, 2, :]), reads=["consts32"], writes=["ones_bf"])
    B.add("act", OP("activation", out=esink[:], in_=colsT[:, C_SINK:C_SINK + 16], func=AF.Exp),
          reads=["colsT"], writes=["esink"])

    if stop == "s":
        return finish()
    m_phase1 = mark()
    xT = sb("xT", [128, KC, NB], BF16)
    attnT = sb("attnT", [128, KC, NB], BF16)
    xThb = sb("xThb", [128, KC, 128], BF16)
    qS = sb("qS", [128, KC, NS], BF16)
    dtmp = [sb(f"dtmp{i}", [128, 512], F32) for i in range(2)]
    m_attn = mark()
    xTha = sb("xTha", [128, KC, 128], BF16)
    kT = sb("kT", [128, 4, NA], BF16)
    Vtok = sb("Vtok", [128, 10, 256], BF16)
    vS32 = sb("vS32", [16, 256], F32)
    cosA = sb("cosA_t", [128, NA], F32)
    sinA = sb("sinA_t", [128, NA], F32)
    cosS = sb("cosS_t", [128, 18], F32)
    sinS = sb("sinS_t", [128, 18], F32)
    masks32 = sb("masks32", [128, 3, 128], F32)
    masks = sb("masks_bf", [128, 3, 128], BF16)
    B.add("sp", OP("dma_start", out=cosA[:], in_=cosA_d), writes=["cosA"], dma="c2")
    B.add("sp", OP("dma_start", out=sinA[:], in_=sinA_d), writes=["sinA"], dma="c3")
    B.add("sp", OP("dma_start", out=cosS[:], in_=cosS_d), writes=["cosS"], dma="c4")
    B.add("sp", OP("dma_start", out=sinS[:], in_=sinS_d), writes=["sinS"], dma="c5")
    B.add("sp", OP("dma_start", out=masks32[:], in_=masks_d), writes=["masks32"], dma="c6")
    B.add("dve", OP("tensor_copy", out=masks[:], in_=masks32[:]), reads=["masks32"], writes=["masks"])

    slot_k, slab_k = load_slab(w_kd, KC, 0, 512, "wk")
    slot_v, slab_v = load_slab(w_v, KC, 0, 256, "wv")

    m_p0 = mark()
    xtok = [sb(f"xtok{i}", [128, D], F32) for i in range(2)]
    xtiles = [(0, 128, [(0, 128, xTha, 0)]),
              (128, 128, [(0, 128, xThb, 0), (126, 2, xT, 1024)])]
    for i in range(8):
        xtiles.append((256 + 128 * i, 128, [(0, 128, xT, 128 * i)]))
    xtiles.append((1280, 16, [(0, 16, xT, 1026)]))
    import os
    if os.environ.get("SKIP_P0"):
        xtiles = []
        B.add("dve", OP("memset", xT[:], 0.5), writes=[("xT", q) for q in range(4)])
        B.add("dve", OP("memset", xTha[:], 0.5), writes=[("xTha", q) for q in range(4)])
        B.add("dve", OP("memset", xThb[:], 0.5), writes=[("xThb", q) for q in range(4)])
    for ti, (r0, nr, dsts) in enumerate(xtiles):
        xb = xtok[ti % 2]
        B.add("sp", OP("dma_start", out=xb[0:nr, :], in_=xin[r0:r0 + nr, :]),
              writes=[("xtok", ti % 2)], dma=f"xtok{ti % 2}")
        for qd in range(4):
            bank = banks[(ti * 4 + qd) % 8]
            bres = ("bk", (ti * 4 + qd) % 8)
            extra = []
            for j in range(4):
                kc = qd * 4 + j
                B.add("pe", OP("transpose",
                    bank[:, j * 128:j * 128 + nr], xb[0:nr, kc * 128:(kc + 1) * 128], ident32[0:nr, 0:nr]),
                    reads=[("xtok", ti % 2), "consts32"], writes=[bres] + extra)
            for (s0, n, dst, dc) in dsts:
                eng = "act" if (qd % 2 == 0) else "dve"
                src = bank[:, :].rearrange("p (j c) -> p j c", j=4)[:, :, s0:s0 + n]
                dd = dst[:, qd * 4:qd * 4 + 4, dc:dc + n]
                dres = ("xT", qd) if dst is xT else (("xTha", qd) if dst is xTha else ("xThb", qd))
                if eng == "act":
                    B.add("act", OP("copy", out=dd, in_=src), reads=[bres], writes=[dres])
                else:
                    B.add("dve", OP("tensor_copy", out=dd, in_=src), reads=[bres], writes=[dres])
    if stop == "0":
        return finish()
    release(m_p0)
    xT_res = [("xT", q) for q in range(4)]
    xTha_res = [("xTha", q) for q in range(4)]
    xThb_res = [("xThb", q) for q in range(4)]

    ropetmp = dict(n=0)
    import os
    ROPE_DBG = int(os.environ.get("ROPE_DBG", "9"))
    raw_bf = [sb(f"raw_bf{i}", [128, 512], BF16) for i in range(4)]
    t1b = [sb(f"t1b{i}", [128, 512], F32) for i in range(4)]
    t2b = [sb(f"t2b{i}", [128, 512], F32) for i in range(2)]

    def rope_chunk(items):
        if ROPE_DBG == 0:
            for it in items:
                rope_piece(*it)
            return
        for i, (bank, off, res, n, cos_ap, sin_ap, dst_ap, dst_res, tab_res) in enumerate(items):
            src = bank[:, off:off + n]
            B.add("act", OP("copy", out=raw_bf[i][:, 0:n], in_=src), reads=[res], writes=[("raw", i)])
        for i, (bank, off, res, n, cos_ap, sin_ap, dst_ap, dst_res, tab_res) in enumerate(items):
            src = bank[:, off:off + n]
            B.add("dve", OP("tensor_tensor", out=t1b[i][:, 0:n], in0=src, in1=cos_ap, op=ALU.mult),
                  reads=[res] + tab_res, writes=[("t1", i)])
        for i, (bank, off, res, n, cos_ap, sin_ap, dst_ap, dst_res, tab_res) in enumerate(items):
            src = bank[:, off:off + n]
            B.add("pe", OP("matmul", src, lhsT=perm_bf[:], rhs=raw_bf[i][:, 0:n], start=True, stop=True),
                  reads=[("raw", i), "perm_bf"], writes=[res])
        for i, (bank, off, res, n, cos_ap, sin_ap, dst_ap, dst_res, tab_res) in enumerate(items):
            src = bank[:, off:off + n]
            t2 = t2b[i % 2]
            B.add("dve", OP("tensor_tensor", out=t2[:, 0:n], in0=src, in1=sin_ap, op=ALU.mult),
                  reads=[res] + tab_res, writes=[("t2", i % 2)])
            B.add("dve", OP("tensor_tensor", out=dst_ap, in0=t1b[i][:, 0:n], in1=t2[:, 0:n], op=ALU.add),
                  reads=[("t1", i), ("t2", i % 2)], writes=[dst_res])


    def rope_piece(bank, off, res, n, cos_ap, sin_ap, dst_ap, dst_res, tab_res):
        if ROPE_DBG == 0:
            B.add("act", OP("copy", out=dst_ap, in_=bank[:, off:off + n]), reads=[res], writes=[dst_res])
            return
        i = ropetmp["n"] % 2
        ropetmp["n"] += 1
        raw, t1, t2 = raw_bf[i], t1b[i], t2b[i]
        src = bank[:, off:off + n]
        B.add("act", OP("copy", out=raw[:, 0:n], in_=src), reads=[res], writes=[("raw", i)])
        B.add("dve", OP("tensor_tensor", out=t1[:, 0:n], in0=src, in1=cos_ap, op=ALU.mult),
              reads=[res] + tab_res, writes=[("t1", i)])
        B.add("pe", OP("matmul", src, lhsT=perm_bf[:], rhs=raw[:, 0:n], start=True, stop=True),
              reads=[("raw", i), "perm_bf"], writes=[res])
        B.add("dve", OP("tensor_tensor", out=t2[:, 0:n], in0=src, in1=sin_ap, op=ALU.mult),
              reads=[res] + tab_res, writes=[("t2", i)])
        B.add("dve", OP("tensor_tensor", out=dst_ap, in0=t1[:, 0:n], in1=t2[:, 0:n], op=ALU.add),
              reads=[("t1", i), ("t2", i)], writes=[dst_res])

    tabA = ["cosA", "sinA"]
    for g in range(4):
        grp = next_group()
        hbi = 13 - grp[2][2][1]
        hb, hres = banks[hbi], ("bk", hbi)
        for hsrc, hsres, hc in ((xTha, xTha_res, 0), (xThb, xThb_res, 128)):
            for k in range(KC):
                B.add("pe", OP("matmul",
                    hb[:, hc:hc + 128], lhsT=slab_k[:, k, g * 128:(g + 1) * 128], rhs=hsrc[:, k, 0:128],
                    start=(k == 0), stop=(k == KC - 1)), reads=[("w", slot_k)] + hsres, writes=[hres])
        for k in range(KC):
            for (bank, off, res), (c0, n) in zip(grp, [(0, 512), (512, 512), (1026, 16)]):
                B.add("pe", OP("matmul",
                    bank[:, off:off + n], lhsT=slab_k[:, k, g * 128:(g + 1) * 128], rhs=xT[:, k, c0:c0 + n],
                    start=(k == 0), stop=(k == KC - 1)), reads=[("w", slot_k)] + xT_res, writes=[res])
        pend_k = [(hb, 0, hres, 256, cosA[:, 0:256], sinA[:, 0:256], kT[:, g, 0:256], ("kT", g), tabA)]
        for (bank, off, res), (c0, n) in zip(grp, [(256, 512), (768, 512), (1280, 16)]):
            pend_k.append((bank, off, res, n, cosA[:, c0:c0 + n], sinA[:, c0:c0 + n], kT[:, g, c0:c0 + n], ("kT", g), tabA))
        rope_chunk(pend_k)
    kT_res = [("kT", g) for g in range(4)]
    if stop == "1a_k":
        return finish()

    vsrc = [(xTha, 0, 128, xTha_res), (xThb, 0, 128, xThb_res)] + [(xT, 128 * i, 128, xT_res) for i in range(8)] + \
           [(xT, 1026, 16, xT_res)]
    for j, (src, c0, n, sres) in enumerate(vsrc):
        bi = j % 2
        bank, bres = banks[bi], ("bk", bi)
        for k in range(KC):
            B.add("pe", OP("matmul",
                bank[0:n, 0:256], lhsT=src[:, k, c0:c0 + n], rhs=slab_v[:, k, 0:256], start=(k == 0), stop=(k == KC - 1)),
                reads=[("w", slot_v)] + sres, writes=[bres])
        if j < 10:
            B.add("act", OP("copy", out=Vtok[:, j, :], in_=bank[:, 0:256]), reads=[bres], writes=[("Vtok", j)])
        else:
            B.add("act", OP("copy", out=vS32[:, :], in_=bank[0:16, 0:256]), reads=[bres], writes=["vS32"])
        if j == 9:
            vst = sb("vst", [128, 256], F32)
            B.add("dve", OP("tensor_copy", out=vst[:], in_=bank[:, 0:256]), reads=[bres], writes=["vst"])
            B.add("sp", OP("dma_start", out=vc_p_o, in_=vst[:]), reads=["vst"], dma="o_vcp")
    B.add("sp", OP("dma_start", out=kv_new_d[1], in_=vS32[:]), reads=["vS32"], writes=["kvnew_v"], dma="kvn_v")

    kst = sb("kst", [128, 256], F32)
    kS32 = sb("kS32", [16, 256], F32)
    tb = banks[2][:, :].bitcast(BF16)
    for g in range(4):
        B.add("pe", OP("transpose", tb[:, g * 128:(g + 1) * 128], kT[:, g, 1152:1280], ident_bf[:]),
              reads=[("kT", g), "ident_bf"], writes=[("bk", 2)])
    B.add("act", OP("copy", out=kst[:].rearrange("p (g d) -> p g d", g=4),
                                  in_=tb[:, 0:512].rearrange("p (g d) -> p g d", g=4)[:, :, 0:64]),
          reads=[("bk", 2)], writes=["kst"])
    B.add("sp", OP("dma_start", out=kc_p_o, in_=kst[:]), reads=["kst"], dma="o_kcp")
    tb3 = banks[3][:, :].bitcast(BF16)
    for g in range(4):
        B.add("pe", OP("transpose", tb3[0:16, g * 128:(g + 1) * 128], kT[:, g, 1280:1296], ident_bf[:]),
              reads=[("kT", g), "ident_bf"], writes=[("bk", 3)])
    B.add("act", OP("copy", out=kS32[:].rearrange("p (g d) -> p g d", g=4),
                                  in_=tb3[0:16, 0:512].rearrange("p (g d) -> p g d", g=4)[:, :, 0:64]),
          reads=[("bk", 3)], writes=["kS32"])
    B.add("sp", OP("dma_start", out=kv_new_d[0], in_=kS32[:]), reads=["kS32"], writes=["kvnew_k"], dma="kvn_k")

    if stop == "1a":
        return finish()
    qT = sb("qT", [128, 4, NB], BF16)
    PT = [sb(f"PT{i}", [128, 512], BF16) for i in range(8)]
    ptn = dict(n=0)
    tabS = ["cosS", "sinS"]

    def attn_S(g, qc0, nq, ktiles, mask_ids, mcol0, blk, half, sbn):
        N = 4 * nq
        hs = slice(half * 64, half * 64 + 64)
        pts = []
        for ki, (kt, mid) in enumerate(zip(ktiles, mask_ids)):
            bi = (sbn % 2) * 2 + ki
            sbk, sres = banks[bi], ("bk", bi)
            pi = (sbn % 4) * 2 + ki
            pt = PT[pi]
            pts.append((ki, kt, pi))
            sout = sbk[:, 0:N].rearrange("p (c q) -> p c q", c=4)
            B.add("pe", OP("matmul", sout, lhsT=kT[hs, g, kt * 128:(kt + 1) * 128], rhs=qT[hs, 0:4, qc0:qc0 + nq], start=True, stop=True),
                  reads=[("kT", g), ("qT", 0)], writes=[sres])
            B.add("act", OP("activation", out=pt[:, 0:N], in_=sbk[:, 0:N], func=AF.Exp, scale=0.125),
                  reads=[sres], writes=[("PT", pi)])
            mk = masks[:, mid, mcol0:mcol0 + nq].unsqueeze(1).broadcast_to([128, 4, nq])
            ptv = pt[:, 0:N].rearrange("p (c q) -> p c q", c=4)
            B.add("dve", OP("tensor_tensor", out=ptv, in0=ptv, in1=mk, op=ALU.mult),
                  reads=[("PT", pi), "masks"], writes=[("PT", pi)])
        return pts

    def attn_PV(g, nq, blk, half, pts):
        N = 4 * nq
        hs = slice(half * 64, half * 64 + 64)
        nbi, dbi = 4 + blk % 2, 6 + blk % 2
        for (ki, kt, pi) in pts:
            B.add("pe", OP("matmul", banks[nbi][hs, 0:N], lhsT=Vtok[:, kt, g * 64:(g + 1) * 64], rhs=PT[pi][:, 0:N], start=(ki == 0), stop=(ki == 1)),
                  reads=[("PT", pi), ("Vtok", kt)], writes=[("bk", nbi)])
        for (ki, kt, pi) in pts:
            B.add("pe", OP("matmul", banks[dbi][hs, 0:N], lhsT=ones_bf[:, 0:64], rhs=PT[pi][:, 0:N], start=(ki == 0), stop=(ki == 1)),
                  reads=[("PT", pi), "ones_bf"], writes=[("bk", dbi)])

    def attn_norm(g, qc0, nq, blk):
        N = 4 * nq
        nbi, dbi = 4 + blk % 2, 6 + blk % 2
        nb, db = banks[nbi], banks[dbi]
        dt_ = dtmp[blk % 2]
        dtr = ("dtmp", blk % 2)
        B.add("dve", OP("tensor_tensor", out=dt_[:, 0:N].rearrange("p (c q) -> p c q", c=4),
                        in0=db[:, 0:N].rearrange("p (c q) -> p c q", c=4),
                        in1=esink[:, 4 * g:4 * g + 4].unsqueeze(2).broadcast_to([128, 4, nq]), op=ALU.add),
              reads=[("bk", dbi), "esink"], writes=[dtr])
        B.add("dve", OP("reciprocal", out=dt_[:, 0:N], in_=dt_[:, 0:N]), reads=[dtr], writes=[dtr])
        B.add("dve", OP("tensor_tensor", out=attnT[:, 4 * g:4 * g + 4, qc0:qc0 + nq],
                        in0=nb[:, 0:N].rearrange("p (c q) -> p c q", c=4),
                        in1=dt_[:, 0:N].rearrange("p (c q) -> p c q", c=4), op=ALU.mult),
              reads=[("bk", nbi), dtr], writes=[("attnT", g)])

    blk = 0
    for g in range(4):
        slot_q, slab_q = load_slab(w_q, KC, 512 * g, 512, "wq")
        pend_rope = None
        for c in range(4):
            grp = next_group()
            mm_acc(grp, slot_q, slab_q, KC, c * 128, 128, xT, xT_res)
            if pend_rope is not None:
                rope_chunk(pend_rope[0])
                B.add("act", OP("copy", out=qS[:, 4 * g + pend_rope[1], :], in_=qT[:, pend_rope[1], 1026:1042]),
                      reads=[("qT", 0)], writes=["qS"])
            items = []
            for pi_, ((bank, off, res), (c0, n)) in enumerate(zip(grp, PIECES_B)):
                if pi_ < 2:
                    items.append((bank, off, res, n, cosA[:, 256 + c0:256 + c0 + n], sinA[:, 256 + c0:256 + c0 + n],
                                  qT[:, c, c0:c0 + n], ("qT", 0), tabA))
                else:
                    items.append((bank, off, res, n, cosS[:, 0:18], sinS[:, 0:18], qT[:, c, c0:c0 + n], ("qT", 0), tabS))
            pend_rope = (items, c)
        rope_chunk(pend_rope[0])
        B.add("act", OP("copy", out=qS[:, 4 * g + pend_rope[1], :], in_=qT[:, pend_rope[1], 1026:1042]),
              reads=[("qT", 0)], writes=["qS"])
        blocks = [(128 * i, 128, (i + 1, i + 2), (2 if i == 0 else 1, 0), 0) for i in range(8)] + [(1024, 2, (0, 1), (1, 0), 126)]
        subs = [(bi_, half) for bi_ in range(len(blocks)) for half in range(2)]
        pend = {}
        for sn in range(len(subs) + 1):
            if sn < len(subs):
                bi_, half = subs[sn]
                qc0, nq, kts, mids, mc0 = blocks[bi_]
                pend[sn] = attn_S(g, qc0, nq, kts, mids, mc0, blk + bi_, half, sn)
            if sn >= 1:
                pb, ph = subs[sn - 1]
                qc0p, nqp = blocks[pb][0], blocks[pb][1]
                attn_PV(g, nqp, blk + pb, ph, pend.pop(sn - 1))
                if ph == 1:
                    attn_norm(g, qc0p, nqp, blk + pb)
        blk += len(blocks)

    if stop == "1b":
        return finish()
    release(m_attn)
    attn_res = [("attnT", g) for g in range(4)]
    m_sa = mark()
    H8 = 8
    Kc = sb("Kc", [128, H8, 256], F32)
    Vc = sb("Vc", [128, H8, 256], F32)
    Kcb = sb("Kcb", [128, H8, 4, 128], BF16)
    Vcb = sb("Vcb", [128, H8, 256], BF16)
    KnT = sb("KnT", [128, H8, 4, 128], BF16)
    PTs = sb("PTs", [128, 256], BF16)
    B.add("sp", OP("dma_start", out=kc_s_o[:, 0:127, :], in_=cache_k[:, 1:128, :]), dma="o_kcs")
    B.add("sp", OP("dma_start", out=vc_s_o[:, 0:127, :], in_=cache_v[:, 1:128, :]), dma="o_vcs")
    B.add("sp", OP("dma_start", out=kc_s_o[:, 127, :], in_=kv_new_d[0]), reads=["kvnew_k"], dma="o_kcs2")
    B.add("sp", OP("dma_start", out=vc_s_o[:, 127, :], in_=kv_new_d[1]), reads=["kvnew_v"], dma="o_vcs2")
    for hh in range(2):
        n0 = hh * H8
        B.add("sp", OP("dma_start", out=Kc[:, :, :], in_=cache_k[n0:n0 + H8, :, :].rearrange("n k f -> k n f")),
              writes=["Kc_a"], dma="kc_a")
        B.add("sp", OP("dma_start", out=Kc[0:1, :, :], in_=kv_new_d[0:1, n0:n0 + H8, :]),
              reads=["kvnew_k", "Kc_a"], writes=["Kc_a", "Kc_b"], dma="kc_b")
        B.add("sp", OP("dma_start", out=Vc[:, :, :], in_=cache_v[n0:n0 + H8, :, :].rearrange("n k f -> k n f")),
              writes=["Vc_a"], dma="vc_a")
        B.add("sp", OP("dma_start", out=Vc[0:1, :, :], in_=kv_new_d[1:2, n0:n0 + H8, :]),
              reads=["kvnew_v", "Vc_a"], writes=["Vc_a", "Vc_b"], dma="vc_b")
        Kc4 = Kc[:].rearrange("p n (g d) -> p n g d", g=4)
        B.add("act", OP("copy", out=Kcb[:, :, :, 0:64], in_=Kc4), reads=["Kc_a", "Kc_b"], writes=["Kcb0"])
        B.add("dve", OP("tensor_copy", out=Kcb[:, :, :, 64:128], in_=Kc4), reads=["Kc_a", "Kc_b"], writes=["Kcb1"])
        B.add("act", OP("copy", out=Vcb[:], in_=Vc[:]), reads=["Vc_a", "Vc_b"], writes=["Vcb"])
        for n in range(H8):
            bi = n % 2
            tbn = banks[bi][:, :].bitcast(BF16)
            for g in range(4):
                B.add("pe", OP("transpose", tbn[:, g * 128:(g + 1) * 128], Kcb[:, n, g, :], ident_bf[:]),
                      reads=["Kcb0", "Kcb1", "ident_bf"], writes=[("bk", bi)])
            srcv = tbn[:, 0:512].rearrange("p (g k) -> p g k", g=4)
            if n % 2 == 0:
                B.add("act", OP("copy", out=KnT[:, n, :, :], in_=srcv), reads=[("bk", bi)], writes=["KnT"])
            else:
                B.add("dve", OP("tensor_copy", out=KnT[:, n, :, :], in_=srcv), reads=[("bk", bi)], writes=["KnT"])
        sbk_h = [(banks[2], ("bk", 2)), (banks[5], ("bk", 5))]
        for n in range(H8):
            for g in range(4):
                for half in range(2):
                    hs = slice(half * 64, half * 64 + 64)
                    sS = sbk_h[half][0][:, 0:128].rearrange("p (g n c) -> p g n c", g=4, n=H8)
                    B.add("pe", OP("matmul", sS[:, g, n, :], lhsT=KnT[hs, n, g, :], rhs=qS[hs, 4 * g:4 * g + 4, n0 + n], start=True, stop=True),
                          reads=["KnT", "qS"], writes=[sbk_h[half][1]])
        for half in range(2):
            B.add("act", OP("activation", out=PTs[:, half * 128:(half + 1) * 128], in_=sbk_h[half][0][:, 0:128], func=AF.Exp, scale=0.125),
                  reads=[sbk_h[half][1]], writes=["PTs"])
        PTv = PTs[:].rearrange("p (h g n c) -> p h g n c", h=2, g=4, n=H8)
        numS = banks[3][:, 0:128].rearrange("p (n c) -> p n c", n=H8)
        denS = banks[4][:, 0:128].rearrange("p (n c) -> p n c", n=H8)
        for n in range(H8):
            for g in range(4):
                for half in range(2):
                    hs = slice(half * 64, half * 64 + 64)
                    B.add("pe", OP("matmul", numS[hs, n, 4 * g:4 * g + 4], lhsT=Vcb[:, n, g * 64:(g + 1) * 64], rhs=PTv[:, half, g, n, :],
                        start=True, stop=True), reads=["PTs", "Vcb"], writes=[("bk", 3)])
                    B.add("pe", OP("matmul", denS[hs, n, 4 * g:4 * g + 4], lhsT=ones_bf[:, 0:64], rhs=PTv[:, half, g, n, :],
                        start=True, stop=True), reads=["PTs", "ones_bf"], writes=[("bk", 4)])
        dS = dtmp[0][:, 0:128].rearrange("p (n c) -> p n c", n=H8)
        B.add("dve", OP("tensor_tensor", out=dS, in0=denS, in1=esink[:, :].unsqueeze(1).broadcast_to([128, H8, 16]), op=ALU.add),
              reads=[("bk", 4), "esink"], writes=[("dtmp", 0)])
        B.add("dve", OP("reciprocal", out=dS, in_=dS), reads=[("dtmp", 0)], writes=[("dtmp", 0)])
        B.add("dve", OP("tensor_tensor", out=attnT[:, :, 1026 + n0:1026 + n0 + H8], in0=numS.rearrange("p n c -> p c n"),
                        in1=dS.rearrange("p n c -> p c n"), op=ALU.mult),
              reads=[("bk", 3), ("dtmp", 0)], writes=attn_res)
    release(m_sa)

    if stop == "1c":
        return finish()
    pooledT = sb("pooledT", [128, 8, NB], BF16)
    m_pool = mark()
    utemps = [sb(f"utemp{i}", [128, 1168], F32) for i in range(2)]
    sA = sb("sA", [128, 1152], F32)
    sBt = sb("sBt", [128, 1152], F32)
    dT = sb("dT", [128, 4, NB], BF16)
    SP = sb("SPst", [120, 2, 1024], F32)
    selT = sb("selT", [120, 4, 8], F32)
    invc = sb("invc", [128, 4, 16], F32)
    tmp16 = sb("tmp16", [128, 16], F32)
    tot16 = sb("tot16", [128, 16], F32)
    uS = sb("uS", [16, 1024], F32)
    uP = sb("uP", [16, 1024], F32)
    B.add("sp", OP("dma_start", out=SP[:], in_=state_pool.rearrange("(a n) h c -> (n h) a c", a=2)), writes=["SP"], dma="sp_l")
    B.add("sp", OP("dma_start", out=selT[:], in_=sel_d), writes=["selT"], dma="sel_l")
    B.add("sp", OP("dma_start", out=invc[:], in_=invc_d), writes=["invc"], dma="invc_l")
    B.add("sp", OP("dma_start", out=pool_s_o[:, 0:14, :], in_=state_pool[:, 1:15, :]), dma="o_pools_a")
    WIN = [2, 4, 8, 16]
    slot_m = take_slot()
    slab_m = wring[slot_m][:, 0:8 * 256].rearrange("p (k c) -> p k c", k=8)
    B.add("pool", OP("dma_start", out=slab_m, in_=w_mix.rearrange("g (kh p) m -> p (g kh) m", p=128)),
          writes=[("w", slot_m)], dma=f"w{slot_m}")
    PIECES_U = [(0, 512), (512, 512), (1026, 16)]

    def pool_small(cc, hb, hres):
        gi = cc // 2
        w = WIN[gi]
        utemp = utemps[cc % 2]
        UT = ("utemp", cc % 2)
        kk = cc % 4
        for a in range(2):
            B.add("pe", OP("matmul", hb[:, 256 + a * 8:256 + a * 8 + 8], lhsT=SP[:, a, cc * 128:(cc + 1) * 128],
                           rhs=selT[:, gi, :], start=True, stop=True), reads=["SP", "selT"], writes=[hres])
        B.add("pe", OP("transpose", hb[0:16, 384:512], utemp[:, 1136:1152], ident32), reads=[UT, "consts32"], writes=[hres])
        B.add("pe", OP("transpose", hb[0:16, 128:256], utemp[:, 1152:1168], ident32), reads=[UT, "consts32"], writes=[hres])
        B.add("dve", OP("tensor_tensor", out=tot16[:], in0=hb[:, 256:272], in1=utemp[:, 1152:1168], op=ALU.add),
              reads=[hres, UT], writes=["tot16"])
        B.add("dve", OP("scalar_tensor_tensor", out=dT[:, kk, 1026:1042], in0=tot16[:], scalar=1.0 / w,
                        in1=utemp[:, 1152:1168], op0=ALU.mult, op1=ALU.subtract), reads=["tot16", UT], writes=[("dT", kk)])
        B.add("act", OP("copy", out=uP[:, cc * 128:(cc + 1) * 128], in_=hb[0:16, 384:512]), reads=[hres], writes=["uP"])
        B.add("act", OP("copy", out=uS[:, cc * 128:(cc + 1) * 128], in_=hb[0:16, 128:256]), reads=[hres], writes=["uS"])

    def pool_mix(gi):
        for oh in range(2):
            grp2 = next_group()
            for kh in range(2):
                ks = (2 * gi + kh) % 4
                for (bank, off, res), (c0, n) in zip(grp2, PIECES_B):
                    B.add("pe", OP("matmul", bank[:, off:off + n], lhsT=slab_m[:, gi * 2 + kh, oh * 128:(oh + 1) * 128],
                                   rhs=dT[:, ks, c0:c0 + n], start=(kh == 0), stop=(kh == 1)),
                          reads=[("w", slot_m), ("dT", ks)], writes=[res])
            oc = 2 * gi + oh
            for (bank, off, res), (c0, n) in zip(grp2, PIECES_B):
                B.add("act", OP("activation", out=pooledT[:, oc, c0:c0 + n], in_=bank[:, off:off + n], func=AF.Copy,
                                scale=colsT[:, C_PSCALE + oc:C_PSCALE + oc + 1]), reads=[res, "colsT"], writes=[("pooledT", oc)])

    for cc in range(8):
        gi = cc // 2
        w = WIN[gi]
        if cc % 4 == 0:
            slot_u, slab_u = load_slab(w_u, KC, 512 * (cc // 4), 512, "wu")
        grp = next_group()
        hbi = 13 - grp[2][2][1]
        hb, hres = banks[hbi], ("bk", hbi)
        mc = (cc % 4) * 128
        utemp = utemps[cc % 2]
        UT = ("utemp", cc % 2)
        for k in range(KC):
            B.add("pe", OP("matmul", hb[:, 0:128], lhsT=slab_u[:, k, mc:mc + 128], rhs=xThb[:, k, 0:128],
                           start=(k == 0), stop=(k == KC - 1)), reads=[("w", slot_u)] + xThb_res, writes=[hres])
            for (bank, off, res), (c0, n) in zip(grp, PIECES_U):
                B.add("pe", OP("matmul", bank[:, off:off + n], lhsT=slab_u[:, k, mc:mc + 128], rhs=xT[:, k, c0:c0 + n],
                               start=(k == 0), stop=(k == KC - 1)), reads=[("w", slot_u)] + xT_res, writes=[res])
        B.add("act", OP("copy", out=utemp[:, 0:128], in_=hb[:, 0:128]), reads=[hres], writes=[UT])
        for (bank, off, res), (d0, n) in zip(grp, [(128, 512), (640, 512), (1152, 16)]):
            B.add("act", OP("copy", out=utemp[:, d0:d0 + n], in_=bank[:, off:off + n]), reads=[res], writes=[UT])
        if cc >= 1:
            pool_small(cc - 1, hb, hres)
            if (cc - 1) % 2 == 1:
                pool_mix((cc - 1) // 2)
        L = 1152
        cur, cur_res, sh = utemp, UT, 1
        bufs = [(sA, "sA"), (sBt, "sBt")]
        bi_ = 0
        lo_ = 0
        while sh < w:
            ob, ores = bufs[bi_ % 2]
            bi_ += 1
            lo_ += sh
            B.add("dve", OP("tensor_tensor", out=ob[:, lo_:L], in0=cur[:, lo_:L], in1=cur[:, lo_ - sh:L - sh], op=ALU.add),
                  reads=[cur_res], writes=[ores])
            cur, cur_res = ob, ores
            sh *= 2
        tot, tot_res = cur, cur_res
        kk = cc % 4
        B.add("dve", OP("scalar_tensor_tensor", out=dT[:, kk, 0:1024], in0=tot[:, 128:1152], scalar=1.0 / w,
                        in1=utemp[:, 128:1152], op0=ALU.mult, op1=ALU.subtract), reads=[tot_res, UT], writes=[("dT", kk)])
        B.add("dve", OP("scalar_tensor_tensor", out=dT[:, kk, 1024:1026], in0=tot[:, 126:128], scalar=1.0 / w,
                        in1=utemp[:, 126:128], op0=ALU.mult, op1=ALU.subtract), reads=[tot_res, UT], writes=[("dT", kk)])
        B.add("dve", OP("tensor_tensor", out=tmp16[:], in0=tot[:, 128:144], in1=invc[:, gi, :], op=ALU.mult),
              reads=[tot_res, "invc"], writes=["tmp16"])
        B.add("dve", OP("tensor_tensor", out=dT[:, kk, 0:16], in0=tmp16[:], in1=utemp[:, 128:144], op=ALU.subtract),
              reads=["tmp16", UT], writes=[("dT", kk)])
    pool_small(7, hb, hres)
    pool_mix(3)
    B.add("sp", OP("dma_start", out=pool_p_o, in_=uP[:]), reads=["uP"], dma="o_poolp")
    B.add("sp", OP("dma_start", out=pool_s_o[:, 14, :], in_=uS[:]), reads=["uS"], dma="o_pools_b")
    pooled_res = [("pooledT", oc) for oc in range(8)]
    release(m_pool)

    if stop == "1d":
        return finish()
    m_1e = mark()
    w4 = sb("wring3", [128, 8192], BF16)
    wring.append(w4)
    wstate["lru"].insert(0, 3)
    sgp = [sb(f"sgp{i}", [128, NB], BF16) for i in range(2)]
    sga = [sb(f"sga{i}", [128, NB], BF16) for i in range(2)]
    m1 = [sb(f"m1_{i}", [128, NB], F32) for i in range(2)]
    m2 = [sb(f"m2_{i}", [128, NB], F32) for i in range(2)]
    mst = [sb(f"mst{i}", [128, NB], BF16) for i in range(2)]
    for j in range(KC):
        if j % 4 == 0:
            slot_gp, slab_gp = load_slab(w_gp, KC, 512 * (j // 4), 512, "wgp")
            slot_ga, slab_ga = load_slab(w_ga, KC, 512 * (j // 4), 512, "wga")
            slot_pb, slab_pb = load_slab(w_pb, 8, 512 * (j // 4), 512, "wpb")
            slot_ab, slab_ab = load_slab(w_ab, KC, 512 * (j // 4), 512, "wab")
        mc = (j % 4) * 128
        i2 = j % 2
        g1 = next_group()
        mm_acc(g1, slot_gp, slab_gp, KC, mc, 128, xT, xT_res)
        for (bank, off, res), (c0, n) in zip(g1, PIECES_B):
            B.add("act", OP("activation", out=sgp[i2][:, c0:c0 + n], in_=bank[:, off:off + n], func=AF.Sigmoid),
                  reads=[res], writes=[("sgp", i2)])
        g2 = next_group()
        mm_acc(g2, slot_ga, slab_ga, KC, mc, 128, xT, xT_res)
        for (bank, off, res), (c0, n) in zip(g2, PIECES_B):
            B.add("act", OP("activation", out=sga[i2][:, c0:c0 + n], in_=bank[:, off:off + n], func=AF.Sigmoid),
                  reads=[res], writes=[("sga", i2)])
        g3 = next_group()
        mm_acc(g3, slot_pb, slab_pb, 8, mc, 128, pooledT, pooled_res)
        for (bank, off, res), (c0, n) in zip(g3, PIECES_B):
            B.add("dve", OP("tensor_tensor", out=m1[i2][:, c0:c0 + n], in0=bank[:, off:off + n],
                                                                                       in1=sgp[i2][:, c0:c0 + n], op=ALU.mult),
                  reads=[res, ("sgp", i2)], writes=[("m1", i2)])
        g4 = next_group()
        mm_acc(g4, slot_ab, slab_ab, KC, mc, 128, attnT, attn_res)
        for (bank, off, res), (c0, n) in zip(g4, PIECES_B):
            B.add("dve", OP("tensor_tensor", out=m2[i2][:, c0:c0 + n], in0=bank[:, off:off + n],
                                                                                       in1=sga[i2][:, c0:c0 + n], op=ALU.mult),
                  reads=[res, ("sga", i2)], writes=[("m2", i2)])
        B.add("dve", OP("tensor_tensor", out=mst[i2][:], in0=m1[i2][:], in1=m2[i2][:], op=ALU.add),
              reads=[("m1", i2), ("m2", i2)], writes=[("mst", i2)])
        B.add("sp", OP("dma_start", out=mg_d[j], in_=mst[i2][:]), reads=[("mst", i2)], writes=[("mg_d", j)], dma=f"mg_st{i2}")
    release(m_1e)
    release(m_phase1)
    wring.pop()
    wstate["lru"].remove(3)

    if stop == "1e":
        return finish()
    m_x1 = mark()
    x1T = sb("x1T", [128, KC, NB], BF16)
    m_p2 = mark()
    yT = sb("yT", [128, KC, NB], F32)
    m_mg = mark()
    mergedT = sb("mergedT", [128, KC, NB], BF16)
    for j in range(KC):
        B.add("sp", OP("dma_start", out=mergedT[:, j, :], in_=mg_d[j]), reads=[("mg_d", j)], writes=[("mergedT", j)], dma=f"mg_ld{j % 4}")
    mg_res = [("mergedT", j) for j in range(KC)]
    for i in range(KC):
        if i % 4 == 0:
            slot_o, slab_o = load_slab(w_o, KC, 512 * (i // 4), 512, "wo")
        grp = next_group()
        mm_acc(grp, slot_o, slab_o, KC, (i % 4) * 128, 128, mergedT, mg_res)
        for pi_, ((bank, off, res), (c0, n)) in enumerate(zip(grp, PIECES_B)):
            if pi_ == 0:
                B.add("act", OP("copy", out=yT[:, i, c0:c0 + n], in_=bank[:, off:off + n]),
                      reads=[res], writes=[("yT", i)])
            else:
                B.add("dve", OP("tensor_copy", out=yT[:, i, c0:c0 + n], in_=bank[:, off:off + n]),
                      reads=[res], writes=[("yT", i)])
    yT_res = [("yT", i) for i in range(KC)]
    release(m_mg)

    if stop == "1f":
        return finish()
    m_ln = mark()
    grep_ = sb("g_rep", [128, D], F32)
    brep_ = sb("b_rep", [128, D], F32)
    xres = [sb(f"xres{i}", [128, D], F32) for i in range(2)]
    ytok = [sb(f"ytok{i}", [128, D], F32) for i in range(2)]
    stats = sb("stats", [128, 4, 6], F32)
    mv = sb("mv", [128, 2], F32)
    rstd = sb("rstd", [128, 1], F32)
    nmr = sb("nmr", [128, 1], F32)
    B.add("sp", OP("dma_start", out=grep_[:], in_=rep_d[0]), writes=["g_rep"], dma="rep_g")
    B.add("sp", OP("dma_start", out=brep_[:], in_=rep_d[1]), writes=["b_rep"], dma="rep_b")

    def layernorm_tile(nt, src_banks, src_res, xr, xr_res, yt, yt_res, from_psum):
        for qd in range(4):
            cs = slice(qd * 512, qd * 512 + 512)
            if from_psum is None:
                pass
            elif from_psum:
                bank, bres = src_banks[qd]
                B.add("dve", OP("scalar_tensor_tensor", out=yt[0:nt, cs], in0=xr[0:nt, cs], scalar=ALPHA,
                                                                                 in1=bank[0:nt, :], op0=ALU.mult, op1=ALU.add),
                      reads=[bres, xr_res], writes=[yt_res, (yt_res, "a"), (yt_res, "b")])
            else:
                B.add("dve", OP("scalar_tensor_tensor", out=yt[0:nt, cs], in0=xr[0:nt, cs], scalar=ALPHA,
                                                                      in1=src_banks[0:nt, cs], op0=ALU.mult, op1=ALU.add),
                      reads=[src_res, xr_res], writes=[yt_res, (yt_res, "a"), (yt_res, "b")])
            B.add("dve", OP("bn_stats", out=stats[0:nt, qd, :], in_=yt[0:nt, cs]), reads=[yt_res], writes=["stats"])
        B.add("dve", OP("bn_aggr", out=mv[0:nt, :], in_=stats[0:nt, :, :].rearrange("p a b -> p (a b)")), reads=["stats"], writes=["mv"])
        B.add("act", OP("activation", out=rstd[0:nt, :], in_=mv[0:nt, 1:2], func=AF.Sqrt, bias=colsT[0:nt, C_EPS:C_EPS + 1], scale=1.0),
              reads=["mv", "colsT"], writes=["rstd"])
        B.add("dve", OP("reciprocal", out=rstd[0:nt, :], in_=rstd[0:nt, :]), reads=["rstd"], writes=["rstd"])
        B.add("dve", OP("scalar_tensor_tensor", out=nmr[0:nt, :], in0=mv[0:nt, 0:1], scalar=-1.0, in1=rstd[0:nt, :], op0=ALU.mult, op1=ALU.mult),
              reads=["mv", "rstd"], writes=["nmr"])
        B.add("act", OP("activation", out=yt[0:nt, :], in_=yt[0:nt, :], func=AF.Identity, bias=nmr[0:nt, :], scale=rstd[0:nt, :]),
              reads=[yt_res, "nmr", "rstd"], writes=[yt_res])
        ya, yb = (yt_res, "a"), (yt_res, "b")
        B.add("pool", OP("tensor_tensor", out=yt[0:nt, 0:1024], in0=yt[0:nt, 0:1024], in1=grep_[0:nt, 0:1024], op=ALU.mult), reads=[yt_res, "g_rep"], writes=[ya])
        B.add("dve", OP("tensor_tensor", out=yt[0:nt, 1024:2048], in0=yt[0:nt, 1024:2048], in1=grep_[0:nt, 1024:2048], op=ALU.mult), reads=[yt_res, "g_rep"], writes=[yb])
        B.add("pool", OP("tensor_tensor", out=yt[0:nt, 0:1024], in0=yt[0:nt, 0:1024], in1=brep_[0:nt, 0:1024], op=ALU.add), reads=[ya, "b_rep"], writes=[ya])
        B.add("dve", OP("tensor_tensor", out=yt[0:nt, 1024:2048], in0=yt[0:nt, 1024:2048], in1=brep_[0:nt, 1024:2048], op=ALU.add), reads=[yb, "b_rep"], writes=[yb])

    ffn_seq = []
    for s_ in range(11):
        ffn_seq.append((w_gate, s_))
        ffn_seq.append((w_up, s_))
    ffn_loaded = []

    def ffn_ensure(upto):
        while len(ffn_loaded) <= min(upto, len(ffn_seq) - 1):
            W_, s_ = ffn_seq[len(ffn_loaded)]
            ffn_loaded.append(load_slab(W_, KC, 512 * s_, 512, "wffn"))

    ffn_ensure(2)
    ttiles = [(128 * i, 128) for i in range(8)] + [(1024, 18)]
    pend_post = None
    for ti, (c0, nt) in enumerate(ttiles):
        xr, xr_res = xres[ti % 2], ("xres", ti % 2)
        yt, yt_res = ytok[ti % 2], ("ytok", ti % 2)
        if nt == 128:
            B.add("sp", OP("dma_start", out=xr[:, :], in_=xin[256 + c0:256 + c0 + 128, :]), writes=[xr_res], dma=f"xres{ti % 2}")
        else:
            B.add("sp", OP("dma_start", out=xr[0:2, :], in_=xin[254:256, :]), writes=[xr_res], dma=f"xres{ti % 2}")
            B.add("sp", OP("dma_start", out=xr[2:18, :], in_=xin[1280:1296, :]), writes=[xr_res], dma=f"xres{ti % 2}b")
        sbk = [(banks[q], ("bk", q)) for q in range(4)]
        for qd in range(4):
            for jj in range(4):
                kc = qd * 4 + jj
                B.add("pe", OP("transpose", banks[qd][0:nt, jj * 128:(jj + 1) * 128],
                                                                                       yT[:, kc, c0:c0 + nt], ident32),
                      reads=[("yT", kc), "consts32"], writes=[("bk", qd)])
        layernorm_tile(nt, sbk, None, xr, xr_res, yt, yt_res, True)
        def post(yt=yt, yt_res=yt_res, c0=c0, nt=nt, ti=ti):
            YR = [yt_res, (yt_res, "a"), (yt_res, "b")]
            B.add("act", OP("dma_start", out=x1_d[c0:c0 + nt, :], in_=yt[0:nt, :]), reads=YR, writes=[("x1_d", ti)], dma=f"x1st{ti % 2}")
            for qd in range(4):
                bq = 4 + qd
                bres = ("bk", bq)
                extra = []
                for jj in range(4):
                    kc = qd * 4 + jj
                    B.add("pe", OP("transpose", banks[bq][:, jj * 128:jj * 128 + nt],
                                                                                          yt[0:nt, kc * 128:(kc + 1) * 128], ident32[0:nt, 0:nt]),
                          reads=YR + ["consts32"], writes=[bres] + extra)
                srcv = banks[bq][:, :].rearrange("p (j c) -> p j c", j=4)[:, :, 0:nt]
                B.add("act", OP("copy", out=x1T[:, qd * 4:qd * 4 + 4, c0:c0 + nt], in_=srcv),
                      reads=[bres], writes=[("x1T", qd)])
        if pend_post is not None:
            pend_post()
        pend_post = post
    pend_post()
    x1T_res = [("x1T", q) for q in range(4)]
    release(m_ln)
    release(m_p2)

    if stop == "2a":
        return finish()
    hT = sb("hT", [128, FC, 1040], BF16)
    m_ffn = mark()
    histT = sb("histT", [128, FC, 32], F32)
    gcol = sb("gcol", [128, FC, 18], F32)
    m_sc = mark()
    SC = sb("SC", [32, DFF], F32)
    B.add("sp", OP("dma_start", out=SC[:], in_=state_conv.rearrange("n h f -> (n h) f")), writes=["SC"], dma="sc_l")
    B.add("sp", OP("dma_start", out=conv_s_o[:, 0, :], in_=state_conv[:, 1, :]), dma="o_convs_a")
    for c in range(FC):
        bq = c % 4
        B.add("pe", OP("transpose", banks[bq][:, 0:32], SC[:, c * 128:(c + 1) * 128], ident32[0:32, 0:32]),
              reads=["SC", "consts32"], writes=[("bk", bq)])
        B.add("act", OP("copy", out=histT[:, c, :], in_=banks[bq][:, 0:32]), reads=[("bk", bq)], writes=["histT"])
    release(m_sc)
    m_gs = mark()
    gs = [sb(f"gs{i}", [128, NB], F32) for i in range(2)]
    cv = [sb(f"cv{i}", [128, 1040], F32) for i in range(1)]
    ge = [sb(f"ge{i}", [128, 1040], BF16) for i in range(1)]
    for c in range(FC):
        if c % 4 == 0:
            ffn_ensure(2 * (c // 4) + 2)
            slot_g, slab_g = ffn_loaded[2 * (c // 4)]
            slot_up, slab_up = ffn_loaded[2 * (c // 4) + 1]
        mc = (c % 4) * 128
        i2 = c % 2
        gsb, gres = gs[i2], ("gs", i2)
        cvb, cres = cv[0], ("cv", 0)
        geb, geres = ge[0], ("ge", 0)
        gg = next_group()
        mm_acc(gg, slot_g, slab_g, KC, mc, 128, x1T, x1T_res)
        (b0, o0, r0), (b1, o1, r1), (b2, o2, r2) = gg
        B.add("act", OP("copy", out=gsb[:, 2:514], in_=b0[:, 0:512]), reads=[r0], writes=[gres])
        B.add("act", OP("copy", out=gsb[:, 514:1026], in_=b1[:, 0:512]), reads=[r1], writes=[gres])
        B.add("act", OP("activation", out=gsb[:, 0:2], in_=b2[:, o2:o2 + 2], func=AF.Copy,
                                                                    scale=colsT[:, C_HV:C_HV + 1]), reads=[r2, "colsT"], writes=[gres])
        B.add("act", OP("copy", out=gsb[:, 1026:1042], in_=b2[:, o2 + 2:o2 + 18]), reads=[r2], writes=[gres])
        gu = next_group()
        mm_acc(gu, slot_up, slab_up, KC, mc, 128, x1T, x1T_res)
        w0 = colsT[:, C_CONVW + c:C_CONVW + c + 1]
        w1 = colsT[:, C_CONVW + FC + c:C_CONVW + FC + c + 1]
        w2 = colsT[:, C_CONVW + 2 * FC + c:C_CONVW + 2 * FC + c + 1]
        cb = colsT[:, C_CONVB + c:C_CONVB + c + 1]
        B.add("dve", OP("tensor_scalar", out=cvb[:, 0:1024], in0=gsb[:, 2:1026], scalar1=w2, scalar2=cb,
                                                                                op0=ALU.mult, op1=ALU.add), reads=[gres, "colsT"], writes=[cres])
        B.add("dve", OP("scalar_tensor_tensor", out=cvb[:, 0:1024], in0=gsb[:, 1:1025], scalar=w1, in1=cvb[:, 0:1024],
                                                                                op0=ALU.mult, op1=ALU.add), reads=[gres, cres, "colsT"], writes=[cres])
        B.add("dve", OP("scalar_tensor_tensor", out=cvb[:, 0:1024], in0=gsb[:, 0:1024], scalar=w0, in1=cvb[:, 0:1024],
                                                                                op0=ALU.mult, op1=ALU.add), reads=[gres, cres, "colsT"], writes=[cres])
        hv = histT[:, c, :].rearrange("p (n h) -> p n h", h=2)
        B.add("dve", OP("tensor_scalar", out=cvb[:, 1024:1040], in0=gsb[:, 1026:1042], scalar1=w2, scalar2=cb,
                                                                                op0=ALU.mult, op1=ALU.add), reads=[gres, "colsT"], writes=[cres])
        B.add("dve", OP("scalar_tensor_tensor", out=cvb[:, 1024:1040], in0=hv[:, :, 1], scalar=w1, in1=cvb[:, 1024:1040],
                                                                              op0=ALU.mult, op1=ALU.add), reads=["histT", cres, "colsT"], writes=[cres])
        B.add("dve", OP("scalar_tensor_tensor", out=cvb[:, 1024:1040], in0=hv[:, :, 0], scalar=w0, in1=cvb[:, 1024:1040],
                                                                              op0=ALU.mult, op1=ALU.add), reads=["histT", cres, "colsT"], writes=[cres])
        B.add("act", OP("activation", out=geb[:, :], in_=cvb[:, :], func=AF.Gelu_apprx_tanh), reads=[cres], writes=[geres])
        B.add("act", OP("copy", out=gcol[:, c, 0:2], in_=gsb[:, 1024:1026]), reads=[gres], writes=["gcol"])
        B.add("act", OP("copy", out=gcol[:, c, 2:18], in_=gsb[:, 1026:1042]), reads=[gres], writes=["gcol"])
        (u0, uo0, ur0), (u1, uo1, ur1), (u2, uo2, ur2) = gu
        B.add("dve", OP("tensor_tensor", out=hT[:, c, 0:512], in0=u0[:, 0:512], in1=geb[:, 0:512], op=ALU.mult),
              reads=[ur0, geres], writes=[("hT", c)])
        B.add("dve", OP("tensor_tensor", out=hT[:, c, 512:1024], in0=u1[:, 0:512], in1=geb[:, 512:1024], op=ALU.mult),
              reads=[ur1, geres], writes=[("hT", c)])
        B.add("dve", OP("tensor_tensor", out=hT[:, c, 1024:1040], in0=u2[:, uo2 + 2:uo2 + 18], in1=geb[:, 1024:1040], op=ALU.mult),
              reads=[ur2, geres], writes=[("hT", c)])
    release(m_gs)
    gst = sb("gst", [18, DFF], F32)
    for c in range(FC):
        bq = c % 4
        B.add("pe", OP("transpose", banks[bq][0:18, 0:128], gcol[:, c, :], ident32), reads=["gcol", "consts32"], writes=[("bk", bq)])
        B.add("act", OP("copy", out=gst[:, c * 128:(c + 1) * 128], in_=banks[bq][0:18, 0:128]), reads=[("bk", bq)], writes=["gst"])
    B.add("sp", OP("dma_start", out=conv_p_o, in_=gst[0:2, :]), reads=["gst"], dma="o_convp")
    B.add("sp", OP("dma_start", out=conv_s_o[:, 1, :], in_=gst[2:18, :]), reads=["gst"], dma="o_convs_b")
    hT_res = [("hT", c) for c in range(FC)]
    release(m_ffn)

    if stop == "2b":
        return finish()
    m_dn = mark()
    ystg = [sb(f"ystg{i}", [128, 128], F32) for i in range(4)]
    x1blk = [sb(f"x1blk{i}", [128, 128], F32) for i in range(8)]
    dtiles = [(128 * i, 128) for i in range(8)] + [(1024, 16)]
    nst = 0
    for jg in range(KC):
        slot_d, slab_d = load_slab(w_down, FC, 128 * jg, 128, "wd")
        for ti, (c0, nt) in enumerate(dtiles):
            bq = nst % 8
            bres = ("bk", bq)
            extra = []
            for c in range(FC):
                B.add("pe", OP("matmul", banks[bq][0:nt, 0:128], lhsT=hT[:, c, c0:c0 + nt],
                                                                                         rhs=slab_d[:, c, 0:128], start=(c == 0), stop=(c == FC - 1)),
                      reads=[("w", slot_d), ("hT", c)] if c in (0, FC - 1) else [("w", slot_d)], writes=[bres] + extra)
            si = nst % 4
            xb = nst % 8
            x1row = c0 if nt == 128 else 1026
            x1tile = ti if nt == 128 else 8
            B.add("sp", OP("dma_start", out=x1blk[xb][0:nt, :], in_=x1_d[x1row:x1row + nt, jg * 128:(jg + 1) * 128]),
                  reads=[("x1_d", x1tile)], writes=[("x1blk", xb)], dma=f"x1blk{xb}")
            B.add("dve", OP("scalar_tensor_tensor", out=ystg[si][0:nt, :], in0=x1blk[xb][0:nt, :], scalar=ALPHA,
                            in1=banks[bq][0:nt, 0:128], op0=ALU.mult, op1=ALU.add),
                  reads=[bres, ("x1blk", xb)], writes=[("ystg", si)])
            B.add("act", OP("dma_start", out=y2_d[c0:c0 + nt, jg * 128:(jg + 1) * 128], in_=ystg[si][0:nt, :]),
                  reads=[("ystg", si)], writes=[("y2_d", ti)], dma=f"y2st{si}")
            nst += 1
    release(m_dn)

    if stop == "2c":
        return finish()
    release(m_x1)
    m_l2 = mark()
    grep_ = sb("g_rep2", [128, D], F32)
    brep_ = sb("b_rep2", [128, D], F32)
    y2r = [sb(f"y2r{i}", [128, D], F32) for i in range(4)]
    stats = sb("stats2", [128, 4, 6], F32)
    mv = sb("mv2", [128, 2], F32)
    rstd = sb("rstd2", [128, 1], F32)
    nmr = sb("nmr2", [128, 1], F32)
    B.add("sp", OP("dma_start", out=grep_[:], in_=rep_d[2]), writes=["g_rep"], dma="rep_g")
    B.add("sp", OP("dma_start", out=brep_[:], in_=rep_d[3]), writes=["b_rep"], dma="rep_b")
    pend_store = []
    for ti, (c0, nt) in enumerate(dtiles):
        y2, y2_res = y2r[ti % 4], ("y2r", ti % 4)
        B.add("sp", OP("dma_start", out=y2[0:nt, :], in_=y2_d[c0:c0 + nt, :]),
              reads=[("y2_d", ti)], writes=[y2_res, (y2_res, "a"), (y2_res, "b")], dma=f"y2ld{ti % 4}")
        layernorm_tile(nt, None, None, None, None, y2, y2_res, None)
        if len(pend_store) >= 2:
            ps_ = pend_store.pop(0)
            B.add("act", *ps_[0], **ps_[1])
        pend_store.append(((OP("dma_start", out=y_o[c0:c0 + nt, :], in_=y2[0:nt, :]),),
                           dict(reads=[y2_res, (y2_res, "a"), (y2_res, "b")], dma=f"o_y{ti % 4}")))
    for ps_ in pend_store:
        B.add("act", *ps_[0], **ps_[1])
    release(m_l2)

    return finish()


def _emit_program(nc, B):
    B.plan()
    sem_cms = {}
    keys = sorted(B.final.keys())
    for k in keys:
        cm = nc.semaphore("s_" + "_".join(str(x) for x in k))
        sem_cms[k] = cm
    sems = {k: cm.__enter__() for k, cm in sem_cms.items()}
    with nc.Block() as block:
        @block.sync
        def _(e):
            B.emit({"sp": e}, sems)

        @block.gpsimd
        def _(e):
            B.emit({"pool": e}, sems)

        @block.tensor
        def _(e):
            B.emit({"pe": e}, sems)

        @block.scalar
        def _(e):
            B.emit({"act": e}, sems)

        @block.vector
        def _(e):
            B.emit({"dve": e}, sems)
    for cm in sem_cms.values():
        cm.__exit__(None, None, None)
    return nc


def _host_tables(core):
    f32 = np.float32
    half = 32
    inv = 10000.0 ** (-np.arange(half, dtype=np.float64) / half)
    posA = np.concatenate([1024 * core - 256 + np.arange(256), 1024 * core + np.arange(1024), np.full(16, 8192)]).astype(np.int64)
    posA = np.maximum(posA, 0).astype(np.float64)
    p = np.arange(128)
    fidx = p % 32
    sign = np.where((p % 64) < 32, -1.0, 1.0)
    ang = posA[None, :] * inv[fidx][:, None]
    cosA = np.cos(ang).astype(f32)
    sinA = (np.sin(ang) * sign[:, None]).astype(f32)
    cosS = np.concatenate([cosA[:, 254:256], cosA[:, 1280:1296]], axis=1)
    sinS = np.concatenate([sinA[:, 254:256], sinA[:, 1280:1296]], axis=1)
    k = np.arange(128)[:, None]
    q = np.arange(128)[None, :]
    m_cur = (k <= q).astype(f32)
    m_prev = (k > q).astype(f32)
    m_prev0 = m_prev if core > 0 else np.zeros_like(m_prev)
    masks = np.stack([m_cur, m_prev, m_prev0], axis=1)
    sel = np.zeros((120, 4, 8), f32)
    for gi, w in enumerate((2, 4, 8, 16)):
        for n in range(8):
            for h in range(15):
                if h >= 16 - w:
                    sel[n * 15 + h, gi, n] = 1.0
    invc = np.zeros((128, 4, 16), f32)
    for gi, w in enumerate((2, 4, 8, 16)):
        for t in range(16):
            cnt = min(w, 1024 * core + t + 1)
            invc[:, gi, t] = 1.0 / cnt
    return cosA, sinA, cosS, sinS, masks, sel, invc


_PROG = {}


def _prep(x_prompt, x_sample, cache_k, cache_v, state_pool, state_conv,
           w_in, attn_sinks, w_pool_mix, pool_scale, w_attn_branch, w_pool_branch, w_out,
           ln1_g, ln1_b, w_up, w_gate, conv_w, conv_b, w_down, ln2_g, ln2_b):
    f32 = np.float32
    A = lambda a: np.ascontiguousarray(np.asarray(a, dtype=f32))
    x_prompt, x_sample = A(x_prompt), A(x_sample)
    w_in = A(w_in)[0]
    wq = A(w_in[:, 0:2048])
    wk = w_in[:, 2048:2304].reshape(D, 4, 1, 64)
    wkd = A(np.broadcast_to(wk, (D, 4, 2, 64)).reshape(D, 512))
    wv = A(w_in[:, 2304:2560])
    wu = A(w_in[:, 2560:3584])
    wgp = A(w_in[:, 3584:5632])
    wga = A(w_in[:, 5632:7680])
    shared = dict(
        w_q=wq, w_kd=wkd, w_v=wv, w_u=wu, w_gp=wgp, w_ga=wga, w_mix=A(w_pool_mix)[0], w_ab=A(w_attn_branch)[0],
        w_pb=A(w_pool_branch)[0], w_o=A(w_out)[0], w_up=A(w_up)[0], w_gate=A(w_gate)[0], w_down=A(w_down)[0],
    )
    rep = A(np.stack([np.broadcast_to(A(v)[0][None, :], (128, D)) for v in (ln1_g, ln1_b, ln2_g, ln2_b)]))
    consts = np.zeros((128, 3, 128), f32)
    consts[:, 0, :] = np.eye(128, dtype=f32)
    pidx = np.arange(128)
    consts[pidx ^ 32, 1, pidx] = 1.0
    consts[:, 2, :] = 1.0
    p = np.arange(128)
    cols_base = np.zeros((128, 256), f32)
    cols_base[:, 0:8] = A(pool_scale)[0].reshape(8, 128).T
    sk = A(attn_sinks)[0]
    for ch in range(16):
        cols_base[:, 8 + ch] = np.where(p >= 64, sk[2 * ch + 1], sk[2 * ch])
    cw = A(conv_w)[0]
    for j in range(3):
        cols_base[:, 24 + j * FC:24 + (j + 1) * FC] = cw[j].reshape(FC, 128).T
    cols_base[:, 156:156 + FC] = A(conv_b)[0].reshape(FC, 128).T
    cols_base[:, 201] = EPS
    xp = x_prompt[0]
    xs = x_sample[:, 0, :]
    ck, cv_ = A(cache_k)[0].reshape(128, 128, 256), A(cache_v)[0].reshape(128, 128, 256)
    spool, sconv = A(state_pool)[0], A(state_conv)[0]
    in_maps = []
    for c in range(NCORES):
        halo = xp[1024 * c - 256:1024 * c] if c > 0 else np.zeros((256, D), f32)
        xin = A(np.concatenate([halo, xp[1024 * c:1024 * (c + 1)], xs[16 * c:16 * (c + 1)]], axis=0))
        cosA, sinA, cosS, sinS, masks, sel, invc = _host_tables(c)
        cols = cols_base.copy()
        cols[:, 200] = 1.0 if c > 0 else 0.0
        m = dict(shared)
        m.update(xin=xin, cache_k=A(ck[16 * c:16 * (c + 1)]), cache_v=A(cv_[16 * c:16 * (c + 1)]),
                 state_pool=A(spool[16 * c:16 * (c + 1)]), state_conv=A(sconv[16 * c:16 * (c + 1)]),
                 cosA=A(cosA), sinA=A(sinA), cosS=A(cosS), sinS=A(sinS), masks=A(masks), sel=sel, invc=invc,
                 cols=cols, rep=rep, consts=consts)
        in_maps.append(m)
    return in_maps


def kernel(**inputs):
    in_maps = _prep(**inputs)
    if "nc" not in _PROG:
        _PROG["nc"] = build_program()
    res = run_bass_kernel_spmd(_PROG["nc"], in_maps, core_ids=list(range(NCORES)))
    return _assemble(res.results)


def _assemble(R):
    y_prompt = np.concatenate([R[c]["y"][0:1024] for c in range(NCORES)], axis=0)[None]
    y_sample = np.concatenate([R[c]["y"][1024:1040] for c in range(NCORES)], axis=0)[:, None, :]
    kc_p = R[7]["kc_p"].reshape(1, 1, 128, 4, 64)
    vc_p = R[7]["vc_p"].reshape(1, 1, 128, 4, 64)
    pool_p = R[7]["pool_p"][1:16].reshape(1, 1, 15, 1024)
    conv_p = R[7]["conv_p"].reshape(1, 1, 2, DFF)
    kc_s = np.concatenate([R[c]["kc_s"] for c in range(NCORES)], axis=0).reshape(1, 128, 128, 4, 64)
    vc_s = np.concatenate([R[c]["vc_s"] for c in range(NCORES)], axis=0).reshape(1, 128, 128, 4, 64)
    pool_s = np.concatenate([R[c]["pool_s"] for c in range(NCORES)], axis=0).reshape(1, 128, 15, 1024)
    conv_s = np.concatenate([R[c]["conv_s"] for c in range(NCORES)], axis=0).reshape(1, 128, 2, DFF)
    outs = (y_prompt, y_sample, kc_p, vc_p, pool_p, conv_p, kc_s, vc_s, pool_s, conv_s)
    return tuple(np.ascontiguousarray(o, dtype=np.float32) for o in outs)
```
